# Optimizing a Trainium2 kernel written in Bass

```python
import math
import jax, jax.numpy as jnp
from jax import lax
import numpy as np

D_MODEL = 4096
BATCH = 4
SEQ = 2048
DEPTH = 2
DEC_BATCH = 128
DEC_SEQ = 1
PAST_LEN = 16384
PAGE_SIZE = 128

MIX_WIDTH = D_MODEL
CONV_CH = MIX_WIDTH // 2
SSM_CH = MIX_WIDTH - CONV_CH
SSM_GROUP = 16
SSM_GROUPS = SSM_CH // SSM_GROUP
SSM_STATE = 64
SSM_CHUNK = 128
CONV_WIDTH = 31
N_MEM = 256
XA_HEADS = 4
XA_HEAD_DIM = 128
XA_WIDTH = XA_HEADS * XA_HEAD_DIM
D_FF = 4 * D_MODEL
EPS = 1e-6

kernel_name = 'hybrid_conv_s5_memxattn_decoder_step'


def _rmsnorm(x, g):
    xf = x.astype(jnp.float32)
    y = xf * lax.rsqrt(jnp.mean(xf * xf, axis=-1, keepdims=True) + EPS)
    return (y * g.astype(jnp.float32)).astype(x.dtype)


def _layernorm(x, g, b):
    xf = x.astype(jnp.float32)
    xc = xf - jnp.mean(xf, axis=-1, keepdims=True)
    var = jnp.mean(xc * xc, axis=-1, keepdims=True)
    return (xc * lax.rsqrt(var + EPS) * g.astype(jnp.float32) + b.astype(jnp.float32)).astype(x.dtype)


def _ssm_discretise(a_re, a_im, log_dt, b_re, b_im, c_re, c_im):
    a_re = a_re.astype(jnp.float32)
    a_im = a_im.astype(jnp.float32)
    b_re = b_re.astype(jnp.float32)
    b_im = b_im.astype(jnp.float32)
    dt = jnp.exp(log_dt.astype(jnp.float32))[:, None]
    mag = jnp.exp(a_re * dt)
    ang = a_im * dt
    abar_re = mag * jnp.cos(ang)
    abar_im = mag * jnp.sin(ang)
    nr = abar_re - 1.0
    ni = abar_im
    den = a_re * a_re + a_im * a_im
    cr = (nr * a_re + ni * a_im) / den
    ci = (ni * a_re - nr * a_im) / den
    bbar_re = cr[..., None] * b_re - ci[..., None] * b_im
    bbar_im = cr[..., None] * b_im + ci[..., None] * b_re
    return (abar_re, abar_im, bbar_re, bbar_im,
            c_re.astype(jnp.float32), c_im.astype(jnp.float32))


def _cmul_combine(e1, e2):
    ar1, ai1, br1, bi1 = e1
    ar2, ai2, br2, bi2 = e2
    return (ar2 * ar1 - ai2 * ai1,
            ar2 * ai1 + ai2 * ar1,
            ar2 * br1 - ai2 * bi1 + br2,
            ar2 * bi1 + ai2 * br1 + bi2)


def _ssm_block(u, h_re, h_im, abar_re, abar_im, bbar_re, bbar_im, c_re, c_im):
    bu_re = jnp.einsum('blgp,gnp->blgn', u, bbar_re)
    bu_im = jnp.einsum('blgp,gnp->blgn', u, bbar_im)
    a_re = jnp.broadcast_to(abar_re, bu_re.shape)
    a_im = jnp.broadcast_to(abar_im, bu_im.shape)
    pr, pi, sr, si = lax.associative_scan(_cmul_combine, (a_re, a_im, bu_re, bu_im), axis=1)
    hr = pr * h_re[:, None] - pi * h_im[:, None] + sr
    hi = pr * h_im[:, None] + pi * h_re[:, None] + si
    y = jnp.einsum('blgn,gpn->blgp', hr, c_re) - jnp.einsum('blgn,gpn->blgp', hi, c_im)
    return y, hr[:, -1], hi[:, -1]


def _ssm_scan(u, h_re, h_im, consts):
    nb, nl = u.shape[0], u.shape[1]
    if nl > SSM_CHUNK and nl % SSM_CHUNK == 0:
        uc = u.reshape(nb, nl // SSM_CHUNK, SSM_CHUNK, SSM_GROUPS, SSM_GROUP).swapaxes(0, 1)

        def step(carry, u_blk):
            y_blk, hr_, hi_ = _ssm_block(u_blk, carry[0], carry[1], *consts)
            return (hr_, hi_), y_blk

        (hr, hi), ys = lax.scan(step, (h_re, h_im), uc)
        return ys.swapaxes(0, 1).reshape(u.shape), hr, hi
    return _ssm_block(u, h_re, h_im, *consts)


def _mixer(h, conv_buf, s_re, s_im, p):
    nb, nl = h.shape[0], h.shape[1]
    z = h @ p['w_in']
    a_val = z[..., :CONV_CH]
    a_gate = z[..., CONV_CH:2 * CONV_CH]
    s_in = z[..., 2 * CONV_CH:]
    g = a_val * jax.nn.sigmoid(a_gate)
    ext = jnp.concatenate([conv_buf.astype(g.dtype), g], axis=1)
    new_buf = ext[:, -(CONV_WIDTH - 1):]
    c = lax.conv_general_dilated(ext, p['conv_w'][:, None, :].astype(g.dtype), (1,), 'VALID',
                                 dimension_numbers=('NWC', 'WIO', 'NWC'),
                                 feature_group_count=CONV_CH) + p['conv_b']
    c = jax.nn.silu(_layernorm(c, p['conv_ln_g'], p['conv_ln_b']))
    consts = _ssm_discretise(p['ssm_a_re'], p['ssm_a_im'], p['ssm_log_dt'],
                             p['ssm_b_re'], p['ssm_b_im'], p['ssm_c_re'], p['ssm_c_im'])
    u = s_in.astype(jnp.float32).reshape(nb, nl, SSM_GROUPS, SSM_GROUP)
    y, hr, hi = _ssm_scan(u, s_re.astype(jnp.float32), s_im.astype(jnp.float32), consts)
    y = (y.reshape(nb, nl, SSM_CH) + p['ssm_d'].astype(jnp.float32) * s_in.astype(jnp.float32)).astype(h.dtype)
    gy = jax.nn.gelu(y)
    s_out = gy * jax.nn.sigmoid(gy @ p['w_glu'])
    merged = jnp.concatenate([_rmsnorm(c, p['branch_g_conv']),
                              _rmsnorm(s_out, p['branch_g_ssm'])], axis=-1)
    return merged @ p['w_out'], new_buf, hr, hi


def _mem_kv(mem, p):
    nb = mem.shape[0]
    m = _rmsnorm(mem, p['norm_mem_g'])
    k = (m @ p['w_xk']).reshape(nb, N_MEM, XA_HEADS, XA_HEAD_DIM)
    v = (m @ p['w_xv']).reshape(nb, N_MEM, XA_HEADS, XA_HEAD_DIM)
    return k, v


def _cross_attn(h, k, v, p):
    nb, nl = h.shape[0], h.shape[1]
    q = (h @ p['w_xq']).reshape(nb, nl, XA_HEADS, XA_HEAD_DIM)
    s = jnp.einsum('blhd,bmhd->bhlm', q, k.astype(q.dtype)).astype(jnp.float32) * (XA_HEAD_DIM ** -0.5)
    w = jax.nn.softmax(s, axis=-1).astype(h.dtype)
    o = jnp.einsum('bhlm,bmhd->blhd', w, v.astype(h.dtype)).reshape(nb, nl, XA_WIDTH)
    return o @ p['w_xo']


def _layer(x, conv_buf, s_re, s_im, mk, mv, p):
    m, new_buf, hr, hi = _mixer(_rmsnorm(x, p['norm_mix_g']), conv_buf, s_re, s_im, p)
    x = x + m
    x = x + _cross_attn(_rmsnorm(x, p['norm_x_g']), mk, mv, p)
    hh = _rmsnorm(x, p['norm_ffn_g'])
    x = x + jnp.square(jax.nn.relu(hh @ p['w_up'])) @ p['w_down']
    return x, new_buf, hr, hi


def setup_inputs(seed: int = 0) -> dict:
    key = jax.random.key(seed)
    ks = iter(jax.random.split(key, 48))
    f32 = jnp.float32

    def nrm(shape, scale):
        return jax.random.normal(next(ks), shape, f32) * scale

    def gain(shape):
        return 1.0 + nrm(shape, 0.02)

    L = DEPTH
    x_prompt = nrm((BATCH, SEQ, D_MODEL), 1.0)
    x_sample = nrm((DEC_BATCH, DEC_SEQ, D_MODEL), 1.0)
    mem_prompt = nrm((BATCH, N_MEM, D_MODEL), 1.0)
    cache_conv = nrm((L, DEC_BATCH, CONV_WIDTH - 1, CONV_CH), 0.5)
    state_ssm_re = nrm((L, DEC_BATCH, SSM_GROUPS, SSM_STATE), 0.1)
    state_ssm_im = nrm((L, DEC_BATCH, SSM_GROUPS, SSM_STATE), 0.1)
    cache_mem_k = nrm((L, DEC_BATCH, N_MEM, XA_HEADS, XA_HEAD_DIM), 1.0)
    cache_mem_v = nrm((L, DEC_BATCH, N_MEM, XA_HEADS, XA_HEAD_DIM), 1.0)
    n_idx = jnp.arange(SSM_STATE, dtype=f32)
    return {
        'x_prompt': x_prompt,
        'x_sample': x_sample,
        'mem_prompt': mem_prompt,
        'cache_conv': cache_conv,
        'state_ssm_re': state_ssm_re,
        'state_ssm_im': state_ssm_im,
        'cache_mem_k': cache_mem_k,
        'cache_mem_v': cache_mem_v,
        'norm_mix_g': gain((L, D_MODEL)),
        'w_in': nrm((L, D_MODEL, 2 * CONV_CH + SSM_CH), D_MODEL ** -0.5),
        'conv_w': nrm((L, CONV_WIDTH, CONV_CH), CONV_WIDTH ** -0.5),
        'conv_b': nrm((L, CONV_CH), 0.02),
        'conv_ln_g': gain((L, CONV_CH)),
        'conv_ln_b': nrm((L, CONV_CH), 0.02),
        'ssm_a_re': -0.5 + nrm((L, SSM_GROUPS, SSM_STATE), 0.01),
        'ssm_a_im': jnp.pi * n_idx + nrm((L, SSM_GROUPS, SSM_STATE), 0.01),
        'ssm_log_dt': jax.random.uniform(next(ks), (L, SSM_GROUPS), f32,
                                         minval=math.log(1e-3), maxval=math.log(1e-1)),
        'ssm_b_re': nrm((L, SSM_GROUPS, SSM_STATE, SSM_GROUP), SSM_GROUP ** -0.5),
        'ssm_b_im': nrm((L, SSM_GROUPS, SSM_STATE, SSM_GROUP), SSM_GROUP ** -0.5),
        'ssm_c_re': nrm((L, SSM_GROUPS, SSM_GROUP, SSM_STATE), (2 * SSM_STATE) ** -0.5),
        'ssm_c_im': nrm((L, SSM_GROUPS, SSM_GROUP, SSM_STATE), (2 * SSM_STATE) ** -0.5),
        'ssm_d': nrm((L, SSM_CH), 1.0),
        'w_glu': nrm((L, SSM_CH, SSM_CH), SSM_CH ** -0.5),
        'branch_g_conv': gain((L, CONV_CH)),
        'branch_g_ssm': gain((L, SSM_CH)),
        'w_out': nrm((L, MIX_WIDTH, D_MODEL), MIX_WIDTH ** -0.5),
        'norm_x_g': gain((L, D_MODEL)),
        'norm_mem_g': gain((L, D_MODEL)),
        'w_xq': nrm((L, D_MODEL, XA_WIDTH), D_MODEL ** -0.5),
        'w_xk': nrm((L, D_MODEL, XA_WIDTH), D_MODEL ** -0.5),
        'w_xv': nrm((L, D_MODEL, XA_WIDTH), D_MODEL ** -0.5),
        'w_xo': nrm((L, XA_WIDTH, D_MODEL), XA_WIDTH ** -0.5),
        'norm_ffn_g': gain((L, D_MODEL)),
        'w_up': nrm((L, D_MODEL, D_FF), D_MODEL ** -0.5),
        'w_down': nrm((L, D_FF, D_MODEL), D_FF ** -0.5),
        'norm_final_g': gain((D_MODEL,)),
    }


def reference(x_prompt, x_sample, mem_prompt, cache_conv, state_ssm_re, state_ssm_im,
              cache_mem_k, cache_mem_v, norm_mix_g, w_in, conv_w, conv_b, conv_ln_g, conv_ln_b,
              ssm_a_re, ssm_a_im, ssm_log_dt, ssm_b_re, ssm_b_im, ssm_c_re, ssm_c_im, ssm_d,
              w_glu, branch_g_conv, branch_g_ssm, w_out, norm_x_g, norm_mem_g, w_xq, w_xk, w_xv,
              w_xo, norm_ffn_g, w_up, w_down, norm_final_g):
    xp = x_prompt
    xs = x_sample
    nbp = x_prompt.shape[0]
    mk_p_all, mv_p_all, cb_p_all, sr_p_all, si_p_all = [], [], [], [], []
    cb_s_all, sr_s_all, si_s_all = [], [], []
    for l in range(DEPTH):
        p = {
            'norm_mix_g': norm_mix_g[l], 'w_in': w_in[l], 'conv_w': conv_w[l], 'conv_b': conv_b[l],
            'conv_ln_g': conv_ln_g[l], 'conv_ln_b': conv_ln_b[l],
            'ssm_a_re': ssm_a_re[l], 'ssm_a_im': ssm_a_im[l], 'ssm_log_dt': ssm_log_dt[l],
            'ssm_b_re': ssm_b_re[l], 'ssm_b_im': ssm_b_im[l], 'ssm_c_re': ssm_c_re[l],
            'ssm_c_im': ssm_c_im[l], 'ssm_d': ssm_d[l], 'w_glu': w_glu[l],
            'branch_g_conv': branch_g_conv[l], 'branch_g_ssm': branch_g_ssm[l], 'w_out': w_out[l],
            'norm_x_g': norm_x_g[l], 'norm_mem_g': norm_mem_g[l], 'w_xq': w_xq[l], 'w_xk': w_xk[l],
            'w_xv': w_xv[l], 'w_xo': w_xo[l], 'norm_ffn_g': norm_ffn_g[l], 'w_up': w_up[l],
            'w_down': w_down[l],
        }
        mk_p, mv_p = _mem_kv(mem_prompt, p)
        zero_buf = jnp.zeros((nbp, CONV_WIDTH - 1, CONV_CH), xp.dtype)
        zero_state = jnp.zeros((nbp, SSM_GROUPS, SSM_STATE), jnp.float32)
        xp, cb_p, sr_p, si_p = _layer(xp, zero_buf, zero_state, zero_state, mk_p, mv_p, p)
        xs, cb_s, sr_s, si_s = _layer(xs, cache_conv[l], state_ssm_re[l], state_ssm_im[l],
                                      cache_mem_k[l], cache_mem_v[l], p)
        mk_p_all.append(mk_p)
        mv_p_all.append(mv_p)
        cb_p_all.append(cb_p)
        sr_p_all.append(sr_p)
        si_p_all.append(si_p)
        cb_s_all.append(cb_s)
        sr_s_all.append(sr_s)
        si_s_all.append(si_s)
    y_prompt = _rmsnorm(xp, norm_final_g)
    y_sample = _rmsnorm(xs, norm_final_g)
    return (y_prompt, y_sample,
            jnp.stack(mk_p_all), jnp.stack(mv_p_all), jnp.stack(cb_p_all),
            jnp.stack(sr_p_all), jnp.stack(si_p_all),
            jnp.stack(cb_s_all), jnp.stack(sr_s_all), jnp.stack(si_s_all))
```

```python
import math
from contextlib import ExitStack
import numpy as np
import concourse.bass as bass
import concourse.mybir as mybir
from concourse.bass_utils import run_bass_kernel_spmd

F32 = mybir.dt.float32
BF16 = mybir.dt.bfloat16
AF = mybir.ActivationFunctionType
ALU = mybir.AluOpType
AX = mybir.AxisListType

D = 4096
ND = 32
CC = 2048
NCC = 16
FF = 16384
L = 2
SEQ = 2048
TT = 256
NS = 16
NMEM = 256
EPS = 1e-6
NSLOT = 3
SLOT_ELEMS = 8192
NDMA = 24
SEM_LIMIT = 20000
PI = math.pi

V_MIX, V_X, V_MEM, V_FFN, V_FINAL = 0, 2, 4, 6, 8
V_CB, V_LNG, V_LNB, V_SD, V_BGC, V_BGS = 0, 2, 4, 6, 8, 10


class Res:
    __slots__ = ("w", "r")

    def __init__(self):
        self.w = None
        self.r = {}


def RL(n):
    return [Res() for _ in range(n)]


class Prog:
    ENG = ["pe", "act", "dve", "pool", "sp"]

    def __init__(self, nc, dry):
        self.nc = nc
        self.dry = dry
        self.q = {e: [] for e in self.ENG}
        self.absq = {e: [] for e in self.ENG}
        self.nsem = 0
        self.es = ExitStack()
        self.pe_sems = set()
        self.cur = {e: [self.new_sem(), 0] for e in ["pe", "act", "dve", "pool"]}
        self.pe_sems.add(id(self.cur["pe"][0]))
        self.pending = {e: False for e in ["pe", "act", "dve", "pool"]}
        self.waited = {e: {} for e in self.ENG}
        self.dma_slots = [[self.new_sem(), 0] for _ in range(NDMA)]
        self.dma_i = 0
        self.ninstr = 0

    def new_sem(self):
        self.nsem += 1
        if self.dry:
            return ("sem", self.nsem)
        return self.es.enter_context(self.nc.semaphore("s%d" % self.nsem))

    def issue(self, eng, fn, reads=(), writes=(), dma=False, mark=True):
        waits = []
        wd = self.waited[eng]

        pe_sems = self.pe_sems

        def need(tok):
            if tok is None:
                return
            sem, val = tok
            if eng == "pe" and id(sem) in pe_sems:
                return
            if wd.get(id(sem), 0) >= val:
                return
            wd[id(sem)] = val
            waits.append((sem, val))

        for r in reads:
            need(r.w)
        for w_ in writes:
            need(w_.w)
            for t in w_.r.values():
                need(t)
        if dma:
            slot = self.dma_slots[self.dma_i % NDMA]
            self.dma_i += 1
            if slot[1] > 0:
                need((slot[0], slot[1]))
            slot[1] += 16
            tok = (slot[0], slot[1])
            inc = (slot[0], 16)
        else:
            c = self.cur[eng]
            if mark:
                if c[1] >= SEM_LIMIT and not self.pending[eng]:
                    c = self.cur[eng] = [self.new_sem(), 0]
                    if eng == "pe":
                        self.pe_sems.add(id(c[0]))
                c[1] += 1
                tok = (c[0], c[1])
                inc = (c[0], 1)
                self.pending[eng] = False
            else:
                tok = (c[0], c[1] + 1)
                inc = None
                self.pending[eng] = True
        for r in reads:
            k = id(tok[0])
            if k not in r.r or r.r[k][1] < tok[1]:
                r.r[k] = tok
        for w_ in writes:
            w_.w = tok
            w_.r = {}
        self.ninstr += 1
        self.absq[eng].append(([(id(a), b) for a, b in waits], None if inc is None else (id(inc[0]), inc[1])))
        if not self.dry:
            def thunk(e, waits=waits, fn=fn, inc=inc):
                for sem, val in waits:
                    e.wait_ge(sem, val)
                ins = fn(e)
                if inc is not None:
                    ins.then_inc(inc[0], inc[1])
            self.q[eng].append(thunk)
        return tok

    def finish(self):
        if self.dry:
            return
        fin = [(s[0], s[1]) for s in self.dma_slots if s[1] > 0]

        def thunk(e):
            for sem, val in fin:
                e.wait_ge(sem, val)
        self.q["sp"].append(thunk)


class Builder:
    def __init__(self, nc, prog, T, dry, wlist):
        self.nc = nc
        self.p = prog
        self.T = T
        self.dry = dry
        self.wlist = wlist
        self.wrec = []
        self.wissued = 0
        self.wget = 0
        self.ps_i = 0
        self.sq_i = 0
        self.misc_i = {}

    def psum(self):
        i = self.ps_i % len(self.T["ps"])
        self.ps_i += 1
        return self.T["ps"][i], self.T["ps_r"][i]

    def rot(self, name):
        lst = self.T[name]
        i = self.misc_i.get(name, 0)
        self.misc_i[name] = i + 1
        return lst[i % len(lst)], self.T[name + "_r"][i % len(lst)]

    def wload_issue(self, k):
        src, a, b = self.wlist[k]
        slot = self.T["wslot"][k % NSLOT]
        res = self.T["wslot_r"][k % NSLOT]
        dst = slot[:, 0:a * b].rearrange("p (a n) -> p a n", a=a)
        self.p.issue("pool", lambda e, dst=dst, src=src: e.dma_start(out=dst, in_=src),
                     writes=[res], dma=True)

    def wget_tile(self, w_ap, r0, a, c0, b):
        src = w_ap[r0:r0 + a * 128, c0:c0 + b].rearrange("(a p) n -> p a n", p=128)
        k = self.wget
        self.wget += 1
        if self.dry:
            self.wrec.append((src, a, b))
        else:
            upto = min(k + NSLOT - 1, len(self.wlist) - 1)
            while self.wissued <= upto:
                self.wload_issue(self.wissued)
                self.wissued += 1
        slot = self.T["wslot"][k % NSLOT]
        return slot[:, 0:a * b].rearrange("p (a n) -> p a n", a=a), self.T["wslot_r"][k % NSLOT]

    def act(self, out, in_, func, reads, writes, bias=None, scale=None, accum=None):
        kw = {}
        if accum is not None:
            kw["accum_out"] = accum
        if bias is not None:
            kw["bias"] = bias
        if scale is not None:
            kw["scale"] = scale
        return self.p.issue("act", lambda e: e.activation(out=out, in_=in_, func=func, **kw),
                            reads=reads, writes=writes)

    def dve(self, fn, reads, writes):
        return self.p.issue("dve", fn, reads=reads, writes=writes)

    def mm(self, out, lhsT, rhs, start, stop, reads, writes, mark):
        return self.p.issue("pe", lambda e: e.matmul(out, lhsT, rhs, start=start, stop=stop),
                            reads=reads, writes=writes, mark=True)

    def tr(self, out, in_, ident, reads, writes):
        return self.p.issue("pe", lambda e: e.transpose(out, in_, ident), reads=reads, writes=writes)

    def dma(self, out, in_, reads, writes):
        return self.p.issue("sp", lambda e: e.dma_start(out=out, in_=in_), reads=reads, writes=writes, dma=True)

    def load_T(self, dstT, dst_res, col0, rows_ap, nrows, nchunks):
        T = self.T
        for h0 in range(0, nchunks, 4):
            nh = min(4, nchunks - h0)
            stg, sr = self.rot("stg")
            self.dma(stg[:nrows, 0:nh * 128], rows_ap[:, h0 * 128:(h0 + nh) * 128], [], [sr])
            for g0 in range(0, nh, 4):
                ps, pr = self.psum()
                for c in range(4):
                    self.tr(ps[:, c * 128:c * 128 + nrows], stg[:nrows, (g0 + c) * 128:(g0 + c + 1) * 128],
                            T["ident"][:nrows, :nrows], [sr, T["const_r"]], [pr])
                o = dstT[:, h0 + g0:h0 + g0 + 4, col0:col0 + nrows]
                i = ps[:, 0:512].rearrange("p (c n) -> p c n", c=4)[:, :, 0:nrows]
                self.act(o, i, AF.Copy, [pr], dst_res[h0 + g0:h0 + g0 + 4])

    def store_T(self, rows_ap, srcT, src_res, col0, nrows, nchunks):
        T = self.T
        for h0 in range(0, nchunks, 4):
            nh = min(4, nchunks - h0)
            stg, sr = self.rot("stg")
            for g0 in range(0, nh, 4):
                ps, pr = self.psum()
                for c in range(4):
                    self.tr(ps[:nrows, c * 128:(c + 1) * 128], srcT[:, h0 + g0 + c, col0:col0 + nrows],
                            T["ident"][:, :], [src_res[h0 + g0 + c], T["const_r"]], [pr])
                self.act(stg[:nrows, g0 * 128:(g0 + 4) * 128], ps[:nrows, 0:512], AF.Copy, [pr], [sr])
            self.dma(rows_ap[:, h0 * 128:(h0 + nh) * 128], stg[:nrows, 0:nh * 128], [sr], [])

    def rstd_bc(self, src, src_res, nch, ncol, out, out_r, nfeat, sub_mean=None):
        T = self.T
        ps, pr = self.psum()
        for c in range(nch):
            sq, sr = self.rot("sq")
            self.act(sq[:, :ncol], src[:, c, :ncol], AF.Square, [src_res[c]], [sr])
            self.mm(ps[:, :ncol], T["ones"][:, :], sq[:, :ncol], c == 0, c == nch - 1,
                    [sr, T["const_r"]], [pr], mark=(c == nch - 1))
        if sub_mean is None:
            self.act(out[:, :ncol], ps[:, :ncol], AF.Sqrt, [pr, T["const_r"]], [out_r],
                     bias=T["eps"][:, 0:1], scale=1.0 / nfeat)
        else:
            mb, mr = sub_mean
            v, vr = self.rot("sq")
            self.dve(lambda e: e.tensor_tensor(out=v[:, :ncol], in0=mb[:, :ncol], in1=mb[:, :ncol], op=ALU.mult),
                     [mr], [vr])
            self.dve(lambda e: e.scalar_tensor_tensor(out=v[:, :ncol], in0=ps[:, :ncol], scalar=1.0 / nfeat,
                                                      in1=v[:, :ncol], op0=ALU.mult, op1=ALU.subtract),
                     [pr, vr], [vr])
            self.act(out[:, :ncol], v[:, :ncol], AF.Sqrt, [vr, T["const_r"]], [out_r], bias=T["eps"][:, 0:1], scale=1.0)
        self.dve(lambda e: e.reciprocal(out=out[:, :ncol], in_=out[:, :ncol]), [out_r], [out_r])

    def norm_to(self, dst, dst_res, src, src_res, nch, ncol, gvec, nfeat):
        T = self.T
        rb, rr = T["rstd"], T["rstd_r"]
        self.rstd_bc(src, src_res, nch, ncol, rb, rr, nfeat)
        for c in range(nch):
            self.dve(lambda e, c=c: e.scalar_tensor_tensor(out=dst[:, c, :ncol], in0=src[:, c, :ncol],
                                                           scalar=gvec[:, c:c + 1], in1=rb[:, :ncol],
                                                           op0=ALU.mult, op1=ALU.mult),
                     [src_res[c], rr, T["const_r"]], [dst_res[c]])

    def proj_group(self, w_tile, w_res, col, nk, rhsT, rhs_res, ncol, kofs=0, ps=None, first=True, last=True):
        if ps is None:
            ps = self.psum()
        pt, pr = ps
        for k in range(nk):
            self.mm(pt[:, :ncol], w_tile[:, k, col * 128:(col + 1) * 128], rhsT[:, kofs + k, :ncol],
                    first and k == 0, last and k == nk - 1,
                    [w_res, rhs_res[kofs + k]], [pr], mark=(last and k == nk - 1))
        return ps

    def proj4(self, w_ap, col0, halves, ncol):
        pss = [self.psum() for _ in range(4)]
        for kh in range(2):
            rhsT, rhs_res, kofs = halves[kh]
            wt, wr = self.wget_tile(w_ap, kh * 2048, 16, col0, 512)
            for o in range(4):
                pt, pr = pss[o]
                for k in range(16):
                    self.mm(pt[:, :ncol], wt[:, k, o * 128:(o + 1) * 128], rhsT[:, kofs + k, :ncol],
                            kh == 0 and k == 0, kh == 1 and k == 15, [wr, rhs_res[kofs + k]], [pr], True)
        return pss

    def build(self, A):
        T = self.T
        p = self.p
        self.p.issue("dve", lambda e: e.memset(T["ones"][:, :], 1.0), [], [T["const_r"]])
        self.p.issue("dve", lambda e: e.memset(T["eps"][:, :], EPS), [], [T["const_r"]])
        self.p.issue("dve", lambda e: e.memset(T["negpi"][:, :], -PI), [], [T["const_r"]])
        self.dma(T["ident"][:, :], A["ident"], [], [T["const_r"]])
        self.dma(T["v4"][:, :], A["vecs4096"], [], [T["const_r"]])
        self.dma(T["v2"][:, :], A["vecs2048"], [], [T["const_r"]])
        self.dma(T["convw"][:, :], A["convw"], [], [T["const_r"]])
        for l in range(L):
            for i in range(NCC):
                self.p.issue("dve", lambda e, l=l, i=i: e.memset(T["hist"][l][:, i, :], 0.0), [], [T["hist_r"][l][i]])
            self.p.issue("dve", lambda e, l=l: e.memset(T["Sre"][l][:, :], 0.0), [], [T["S_r"][l]])
            self.p.issue("dve", lambda e, l=l: e.memset(T["Sim"][l][:, :], 0.0), [], [T["S_r"][l]])
        for q4 in range(4):
            for nm in ("tB_re", "tB_im", "tC_re", "tC_im"):
                self.p.issue("dve", lambda e, nm=nm, q4=q4: e.memset(T[nm][q4][:, :], 0.0), [], [T[nm + "_r"][q4]])

        self.mem_kv(A)
        tiles = [("p", t0, TT) for t0 in range(0, SEQ, TT)] + [("s", 0, NS)]
        for (kind, t0, ncol) in tiles:
            xT, xr = T["xT"], T["xT_r"]
            if kind == "p":
                for r0 in range(0, ncol, 128):
                    self.load_T(xT, xr, r0, A["xp"][t0 + r0:t0 + r0 + 128, :], 128, ND)
            else:
                self.load_T(xT, xr, 0, A["xs"][:, :], NS, ND)
            for l in range(L):
                self.layer(A, l, kind, t0, ncol)
            self.final_out(A, kind, t0, ncol)
        p.finish()

    def mem_kv(self, A):
        T = self.T
        xT, xr = T["xT"], T["xT_r"]
        hT, hr = T["hT"], T["hT_r"]
        for r0 in range(0, NMEM, 128):
            self.load_T(xT, xr, r0, A["mem"][r0:r0 + 128, :], 128, ND)
        for l in range(L):
            g = T["v4"][:, (V_MEM + l) * ND:(V_MEM + l + 1) * ND]
            self.norm_to(hT, hr, xT, xr, ND, NMEM, g, D)
            for hh in range(2):
                wt, wr = self.wget_tile(A["w_xk"][l], 0, 32, hh * 256, 256)
                for c2 in range(2):
                    h = hh * 2 + c2
                    ps = self.proj_group(wt, wr, c2, 32, hT, hr, NMEM)
                    self.act(T["KT"][l][:, h, :], ps[0][:, :NMEM], AF.Copy, [ps[1]], [T["KT_r"][l]])
            for which, wname, oname in ((0, "w_xk", "mk"), (1, "w_xv", "mv")):
                pss = [self.psum(), self.psum()]
                for half in range(2):
                    wt, wr = self.wget_tile(A[wname][l], half * 2048, 16, 0, 512)
                    for mb in range(2):
                        pt, pr = pss[mb]
                        for k in range(16):
                            c = half * 16 + k
                            self.mm(pt[:, :512], hT[:, c, mb * 128:(mb + 1) * 128], wt[:, k, :],
                                    c == 0, c == 31, [wr, hr[c]], [pr], mark=(k == 15))
                for mb in range(2):
                    pt, pr = pss[mb]
                    stg, sr = self.rot("stg")
                    self.act(stg[:, 0:512], pt[:, :512], AF.Copy, [pr], [sr])
                    self.dma(A[oname][l, mb * 128:(mb + 1) * 128, :], stg[:, 0:512], [sr], [])
                    if which == 1:
                        self.act(T["Vb"][l][:, mb, :], pt[:, :512], AF.Copy, [pr], [T["Vb_r"][l]])

    def layer(self, A, l, kind, t0, ncol):
        T = self.T
        xT, xr = T["xT"], T["xT_r"]
        hT, hr = T["hT"], T["hT_r"]
        self.norm_to(hT, hr, xT, xr, ND, ncol, T["v4"][:, (V_MIX + l) * ND:(V_MIX + l + 1) * ND], D)
        self.conv_branch(A, l, kind, t0, ncol)
        self.ssm_consts(A, l)
        self.ssm_branch(A, l, kind, t0, ncol)
        M1, M1r, M2, M2r = T["M1"], T["M1_r"], T["M2"], T["M2_r"]
        for dg in range(0, ND, 4):
            pss = self.proj4(A["w_out"][l], dg * 128, [(M1, M1r, 0), (M2, M2r, 0)], ncol)
            for d4 in range(4):
                d = dg + d4
                pt, pr = pss[d4]
                self.dve(lambda e, d=d, pt=pt: e.tensor_tensor(out=xT[:, d, :ncol], in0=pt[:, :ncol], in1=xT[:, d, :ncol],
                                                              op=ALU.add), [pr, xr[d]], [xr[d]])
        self.xattn(A, l, kind, t0, ncol)
        self.mlp(A, l, ncol)

    def ssm_consts(self, A, l):
        T = self.T
        cr_ = T["sc_r"]
        sc = T["sc"]
        self.dma(sc["are"][:, :], A["a_re"][:, l, :], [], [cr_])
        self.dma(sc["aim"][:, :], A["a_im"][:, l, :], [], [cr_])
        self.dma(sc["dt"][:, :], A["log_dt"][:, l, :], [], [cr_])
        R = [cr_]
        a = lambda out, in_, func, **kw: self.act(out, in_, func, R, R, **kw)
        v = lambda fn: self.dve(fn, R, R)
        a(sc["dt"][:, :], sc["dt"][:, :], AF.Exp)
        v(lambda e: e.tensor_tensor(out=sc["t0"][:, :], in0=sc["are"][:, :], in1=sc["dt"][:, :], op=ALU.mult))
        a(sc["mag"][:, :], sc["t0"][:, :], AF.Exp)
        v(lambda e: e.tensor_tensor(out=sc["ang"][:, :], in0=sc["aim"][:, :], in1=sc["dt"][:, :], op=ALU.mult))
        ki = T["sc_i"]

        def reduce_to(dst, src_fn):
            src_fn(sc["t1"])
            v(lambda e: e.tensor_scalar(out=sc["t0"][:, :], in0=sc["t1"][:, :], scalar1=1.0 / (2 * PI), scalar2=None, op0=ALU.mult))
            v(lambda e: e.tensor_copy(out=ki[:, :], in_=sc["t0"][:, :]))
            v(lambda e: e.tensor_copy(out=sc["t0"][:, :], in_=ki[:, :]))
            v(lambda e: e.scalar_tensor_tensor(out=dst[:, :], in0=sc["t0"][:, :], scalar=-2 * PI, in1=sc["t1"][:, :],
                                               op0=ALU.mult, op1=ALU.add))
            v(lambda e: e.tensor_scalar(out=sc["t0"][:, :], in0=dst[:, :], scalar1=PI, scalar2=None, op0=ALU.is_gt))
            v(lambda e: e.scalar_tensor_tensor(out=dst[:, :], in0=sc["t0"][:, :], scalar=-2 * PI, in1=dst[:, :],
                                               op0=ALU.mult, op1=ALU.add))
            v(lambda e: e.tensor_scalar(out=sc["t0"][:, :], in0=dst[:, :], scalar1=-PI, scalar2=None, op0=ALU.is_lt))
            v(lambda e: e.scalar_tensor_tensor(out=dst[:, :], in0=sc["t0"][:, :], scalar=2 * PI, in1=dst[:, :],
                                               op0=ALU.mult, op1=ALU.add))
        reduce_to(sc["sin"], lambda t: v(lambda e: e.tensor_copy(out=t[:, :], in_=sc["ang"][:, :])))
        a(sc["sin"][:, :], sc["sin"][:, :], AF.Sin)
        reduce_to(sc["cos"], lambda t: v(lambda e: e.tensor_scalar(out=t[:, :], in0=sc["ang"][:, :], scalar1=0.5 * PI, scalar2=None,
                                                                  op0=ALU.add)))
        a(sc["cos"][:, :], sc["cos"][:, :], AF.Sin)
        P = T["Apow"]
        v(lambda e: e.tensor_tensor(out=P[:, :, 0, 0], in0=sc["mag"][:, :], in1=sc["cos"][:, :], op=ALU.mult))
        v(lambda e: e.tensor_tensor(out=P[:, :, 0, 1], in0=sc["mag"][:, :], in1=sc["sin"][:, :], op=ALU.mult))
        v(lambda e: e.tensor_scalar(out=sc["nr"][:, :], in0=P[:, :, 0, 0], scalar1=-1.0, scalar2=None, op0=ALU.add))
        v(lambda e: e.tensor_tensor(out=sc["t0"][:, :], in0=sc["are"][:, :], in1=sc["are"][:, :], op=ALU.mult))
        v(lambda e: e.tensor_tensor(out=sc["t1"][:, :], in0=sc["aim"][:, :], in1=sc["aim"][:, :], op=ALU.mult))
        v(lambda e: e.tensor_tensor(out=sc["t0"][:, :], in0=sc["t0"][:, :], in1=sc["t1"][:, :], op=ALU.add))
        v(lambda e: e.reciprocal(out=sc["den"][:, :], in_=sc["t0"][:, :]))
        v(lambda e: e.tensor_tensor(out=sc["t0"][:, :], in0=sc["nr"][:, :], in1=sc["are"][:, :], op=ALU.mult))
        v(lambda e: e.tensor_tensor(out=sc["t1"][:, :], in0=P[:, :, 0, 1], in1=sc["aim"][:, :], op=ALU.mult))
        v(lambda e: e.tensor_tensor(out=sc["t0"][:, :], in0=sc["t0"][:, :], in1=sc["t1"][:, :], op=ALU.add))
        v(lambda e: e.tensor_tensor(out=sc["cr"][:, :], in0=sc["t0"][:, :], in1=sc["den"][:, :], op=ALU.mult))
        v(lambda e: e.tensor_tensor(out=sc["t0"][:, :], in0=P[:, :, 0, 1], in1=sc["are"][:, :], op=ALU.mult))
        v(lambda e: e.tensor_tensor(out=sc["t1"][:, :], in0=sc["nr"][:, :], in1=sc["aim"][:, :], op=ALU.mult))
        v(lambda e: e.tensor_tensor(out=sc["t0"][:, :], in0=sc["t0"][:, :], in1=sc["t1"][:, :], op=ALU.subtract))
        v(lambda e: e.tensor_tensor(out=sc["ci"][:, :], in0=sc["t0"][:, :], in1=sc["den"][:, :], op=ALU.mult))
        for k in range(9):
            if k > 0:
                v(lambda e, k=k: e.tensor_tensor(out=sc["t0"][:, :], in0=P[:, :, k - 1, 0], in1=P[:, :, k - 1, 0], op=ALU.mult))
                v(lambda e, k=k: e.tensor_tensor(out=sc["t1"][:, :], in0=P[:, :, k - 1, 1], in1=P[:, :, k - 1, 1], op=ALU.mult))
                v(lambda e, k=k: e.tensor_tensor(out=P[:, :, k, 1], in0=P[:, :, k - 1, 0], in1=P[:, :, k - 1, 1], op=ALU.mult))
                v(lambda e, k=k: e.tensor_tensor(out=P[:, :, k, 0], in0=sc["t0"][:, :], in1=sc["t1"][:, :], op=ALU.subtract))
                v(lambda e, k=k: e.tensor_scalar(out=P[:, :, k, 1], in0=P[:, :, k, 1], scalar1=2.0, scalar2=None, op0=ALU.mult))
            v(lambda e, k=k: e.tensor_scalar(out=P[:, :, k, 2], in0=P[:, :, k, 1], scalar1=-1.0, scalar2=None, op0=ALU.mult))

    def ssm_branch(self, A, l, kind, t0, ncol):
        T = self.T
        hT, hr = T["hT"], T["hT_r"]
        SA, SAr = T["SA"], T["SA_r"]
        P = T["Apow"]
        RC = [T["sc_r"]]
        if kind == "s":
            for a4 in range(4):
                self.load_T(T["SsR"][:, a4], T["SsR_r"][0:16], 0, A["st_re"][l][:, a4 * 2048:(a4 + 1) * 2048], NS, 16)
                self.load_T(T["SsI"][:, a4], T["SsI_r"][0:16], 0, A["st_im"][l][:, a4 * 2048:(a4 + 1) * 2048], NS, 16)
        for jg in range(0, NCC, 4):
            pss4 = self.proj4(A["w_in"][l], 2 * CC + jg * 128, [(hT, hr, 0), (hT, hr, 16)], ncol)
            Us = []
            for j2 in range(4):
                U, Ur = self.rot("U")
                self.act(U[:, :ncol], pss4[j2][0][:, :ncol], AF.Copy, [pss4[j2][1]], [Ur])
                Us.append((U, Ur))
            for j2 in range(4):
                j = jg + j2
                U, Ur = Us[j2]
                bR, bRr = self.rot("bc")
                bI, bIr = self.rot("bc")
                t0_, t0r = self.rot("bc")
                t1_, t1r = self.rot("bc")
                cR, cRr = self.rot("cc")
                cI, cIr = self.rot("cc")
                v3 = lambda t: t[:, :].rearrange("p (a b) -> p a b", a=4)
                self.dma(v3(bR), A["b_re"][:, l, 4 * j:4 * j + 4, :], [], [bRr])
                self.dma(v3(bI), A["b_im"][:, l, 4 * j:4 * j + 4, :], [], [bIr])
                self.dma(v3(cR), A["c_re"][:, l, 4 * j:4 * j + 4, :], [], [cRr])
                self.dma(v3(cI), A["c_im"][:, l, 4 * j:4 * j + 4, :], [], [cIr])
                crb = T["sc"]["cr"][:, 4 * j:4 * j + 4].unsqueeze(2).broadcast_to([128, 4, 32])
                cib = T["sc"]["ci"][:, 4 * j:4 * j + 4].unsqueeze(2).broadcast_to([128, 4, 32])
                self.dve(lambda e, t0_=t0_, bR=bR, crb=crb: e.tensor_tensor(out=v3(t0_), in0=v3(bR), in1=crb, op=ALU.mult), [bRr] + RC, [t0r])
                self.dve(lambda e, t1_=t1_, bI=bI, cib=cib: e.tensor_tensor(out=v3(t1_), in0=v3(bI), in1=cib, op=ALU.mult), [bIr] + RC, [t1r])
                self.dve(lambda e, t0_=t0_, t1_=t1_: e.tensor_tensor(out=t0_[:, :], in0=t0_[:, :], in1=t1_[:, :], op=ALU.subtract), [t0r, t1r], [t0r])
                self.dve(lambda e, t1_=t1_, bI=bI, crb=crb: e.tensor_tensor(out=v3(t1_), in0=v3(bI), in1=crb, op=ALU.mult), [bIr] + RC, [t1r])
                self.dve(lambda e, bR=bR, cib=cib: e.tensor_tensor(out=v3(bR), in0=v3(bR), in1=cib, op=ALU.mult), [bRr] + RC, [bRr])
                self.dve(lambda e, t1_=t1_, bR=bR: e.tensor_tensor(out=t1_[:, :], in0=t1_[:, :], in1=bR[:, :], op=ALU.add), [t1r, bRr], [t1r])
                self.dve(lambda e, cI=cI: e.tensor_scalar(out=cI[:, :], in0=cI[:, :], scalar1=-1.0, scalar2=None, op0=ALU.mult), [cIr], [cIr])
                for (src, srcr, tname) in ((t0_, t0r, "tB_re"), (t1_, t1r, "tB_im")):
                    pt, pr = self.psum()
                    self.tr(pt[:, 0:128], src[:, :], T["ident"][:, :], [srcr, T["const_r"]], [pr])
                    for q4 in range(4):
                        self.act(T[tname][q4][32 * q4:32 * q4 + 32, :], pt[32 * q4:32 * q4 + 32, 0:128], AF.Copy,
                                 [pr], [T[tname + "_r"][q4]])
                yps = (T["psy"][0], T["psy_r"][0])
                for q4 in range(4):
                    q = 4 * j + q4
                    pre = self.psum()
                    pim = self.psum()
                    self.mm(pre[0][:, :ncol], T["tB_re"][q4][:, :], U[:, :ncol], True, True, [T["tB_re_r"][q4], Ur], [pre[1]], True)
                    self.mm(pim[0][:, :ncol], T["tB_im"][q4][:, :], U[:, :ncol], True, True, [T["tB_im_r"][q4], Ur], [pim[1]], True)
                    are = lambda k: P[:, q, k, 0:1]
                    aim = lambda k: P[:, q, k, 1:2]
                    nim = lambda k: P[:, q, k, 2:3]
                    if kind == "p":
                        hs, hsr = self.rot("HS")
                        n1 = ncol + 1
                        self.act(hs[:, 0, 1:n1], pre[0][:, :ncol], AF.Copy, [pre[1]], [hsr])
                        self.act(hs[:, 1, 1:n1], pim[0][:, :ncol], AF.Copy, [pim[1]], [hsr])
                        self.act(hs[:, 0, 0:1], T["Sre"][l][:, q:q + 1], AF.Copy, [T["S_r"][l]], [hsr])
                        self.act(hs[:, 1, 0:1], T["Sim"][l][:, q:q + 1], AF.Copy, [T["S_r"][l]], [hsr])
                        cur = 0
                        k = 0
                        d = 1
                        while d < n1:
                            o = 2 - cur
                            r0, i0, r1, i1 = hs[:, cur, :], hs[:, cur + 1, :], hs[:, o, :], hs[:, o + 1, :]
                            W = [hsr] + RC
                            sa, sn, si = are(k), nim(k), aim(k)
                            heng = "dve"
                            hv = lambda fn: self.p.issue(heng, fn, reads=W, writes=[hsr])
                            hv(lambda e, r0=r0, r1=r1, sa=sa, d=d: e.scalar_tensor_tensor(
                                out=r1[:, d:n1], in0=r0[:, 0:n1 - d], scalar=sa, in1=r0[:, d:n1], op0=ALU.mult, op1=ALU.add))
                            hv(lambda e, i0=i0, r1=r1, sn=sn, d=d: e.scalar_tensor_tensor(
                                out=r1[:, d:n1], in0=i0[:, 0:n1 - d], scalar=sn, in1=r1[:, d:n1], op0=ALU.mult, op1=ALU.add))
                            hv(lambda e, i0=i0, i1=i1, sa=sa, d=d: e.scalar_tensor_tensor(
                                out=i1[:, d:n1], in0=i0[:, 0:n1 - d], scalar=sa, in1=i0[:, d:n1], op0=ALU.mult, op1=ALU.add))
                            hv(lambda e, r0=r0, i1=i1, si=si, d=d: e.scalar_tensor_tensor(
                                out=i1[:, d:n1], in0=r0[:, 0:n1 - d], scalar=si, in1=i1[:, d:n1], op0=ALU.mult, op1=ALU.add))
                            self.act(r1[:, 0:d], r0[:, 0:d], AF.Copy, [hsr], [hsr])
                            self.act(i1[:, 0:d], i0[:, 0:d], AF.Copy, [hsr], [hsr])
                            cur = o
                            k += 1
                            d *= 2
                        fr, fi = hs[:, cur, :], hs[:, cur + 1, :]
                        self.act(T["Sre"][l][:, q:q + 1], fr[:, ncol:n1], AF.Copy, [hsr], [T["S_r"][l]])
                        self.act(T["Sim"][l][:, q:q + 1], fi[:, ncol:n1], AF.Copy, [hsr], [T["S_r"][l]])
                        hre, him = fr[:, 1:n1], fi[:, 1:n1]
                    else:
                        hs, hsr = self.rot("hss")
                        sr_, si_ = T["SsR"][:, q // 16, q % 16, 0:NS], T["SsI"][:, q // 16, q % 16, 0:NS]
                        W = [hsr, T["SsR_r"][q], T["SsI_r"][q]] + RC
                        hre, him = hs[:, 0, 0:NS], hs[:, 1, 0:NS]
                        sa, sn, si = are(0), nim(0), aim(0)
                        self.dve(lambda e, hre=hre, sr_=sr_, pre=pre, sa=sa: e.scalar_tensor_tensor(
                            out=hre, in0=sr_, scalar=sa, in1=pre[0][:, :NS], op0=ALU.mult, op1=ALU.add), W + [pre[1]], [hsr])
                        self.dve(lambda e, hre=hre, si_=si_, sn=sn: e.scalar_tensor_tensor(
                            out=hre, in0=si_, scalar=sn, in1=hre, op0=ALU.mult, op1=ALU.add), W, [hsr])
                        self.dve(lambda e, him=him, si_=si_, pim=pim, sa=sa: e.scalar_tensor_tensor(
                            out=him, in0=si_, scalar=sa, in1=pim[0][:, :NS], op0=ALU.mult, op1=ALU.add), W + [pim[1]], [hsr])
                        self.dve(lambda e, him=him, sr_=sr_, si=si: e.scalar_tensor_tensor(
                            out=him, in0=sr_, scalar=si, in1=him, op0=ALU.mult, op1=ALU.add), W, [hsr])
                        self.act(T["SsR"][:, q // 16, q % 16, 0:NS], hre, AF.Copy, [hsr], [T["SsR_r"][q]])
                        self.act(T["SsI"][:, q // 16, q % 16, 0:NS], him, AF.Copy, [hsr], [T["SsI_r"][q]])
                    self.act(T["tC_re"][q4][:, 32 * q4:32 * q4 + 32], cR[:, 32 * q4:32 * q4 + 32], AF.Copy, [cRr], [T["tC_re_r"][q4]])
                    self.act(T["tC_im"][q4][:, 32 * q4:32 * q4 + 32], cI[:, 32 * q4:32 * q4 + 32], AF.Copy, [cIr], [T["tC_im_r"][q4]])
                    self.mm(yps[0][:, :ncol], T["tC_re"][q4][:, :], hre, q4 == 0, False, [T["tC_re_r"][q4], hsr], [yps[1]], False)
                    self.mm(yps[0][:, :ncol], T["tC_im"][q4][:, :], him, False, q4 == 3, [T["tC_im_r"][q4], hsr], [yps[1]], q4 == 3)
                dvec = T["v2"][:, (V_SD + l) * NCC + j:(V_SD + l) * NCC + j + 1]
                yt, ytr = self.rot("sq")
                self.dve(lambda e, yt=yt, U=U, dvec=dvec, yps=yps: e.scalar_tensor_tensor(
                    out=yt[:, :ncol], in0=U[:, :ncol], scalar=dvec, in1=yps[0][:, :ncol], op0=ALU.mult, op1=ALU.add),
                    [Ur, yps[1], T["const_r"]], [ytr])
                g0, g0r = self.rot("sq")
                self.act(g0[:, :ncol], yt[:, :ncol], AF.Square, [ytr], [g0r])
                self.dve(lambda e, g0=g0: e.tensor_scalar(out=g0[:, :ncol], in0=g0[:, :ncol], scalar1=0.044715, scalar2=1.0,
                                                         op0=ALU.mult, op1=ALU.add), [g0r], [g0r])
                self.dve(lambda e, g0=g0, yt=yt: e.tensor_tensor(out=g0[:, :ncol], in0=g0[:, :ncol], in1=yt[:, :ncol], op=ALU.mult),
                         [g0r, ytr], [g0r])
                self.act(g0[:, :ncol], g0[:, :ncol], AF.Tanh, [g0r], [g0r], scale=math.sqrt(2.0 / PI))
                self.dve(lambda e, g0=g0, yt=yt: e.scalar_tensor_tensor(out=g0[:, :ncol], in0=g0[:, :ncol], scalar=1.0, in1=yt[:, :ncol],
                                                                       op0=ALU.add, op1=ALU.mult), [g0r, ytr], [g0r])
                self.act(SA[:, j, :ncol], g0[:, :ncol], AF.Copy, [g0r], [SAr[j]], scale=0.5)
        if kind == "s":
            for a4 in range(4):
                self.store_T(A["sr_s"][l][:, a4 * 2048:(a4 + 1) * 2048], T["SsR"][:, a4], T["SsR_r"][0:16], 0, NS, 16)
                self.store_T(A["si_s"][l][:, a4 * 2048:(a4 + 1) * 2048], T["SsI"][:, a4], T["SsI_r"][0:16], 0, NS, 16)
        elif t0 + ncol == SEQ:
            for (src, oname) in ((T["Sre"][l], "sr_p"), (T["Sim"][l], "si_p")):
                pt, pr = self.psum()
                self.tr(pt[:64, 0:128], src[:, :], T["ident"][:, :], [T["S_r"][l], T["const_r"]], [pr])
                stg, sr = self.rot("stg")
                self.act(stg[:64, 0:128], pt[:64, 0:128], AF.Copy, [pr], [sr])
                self.dma(A[oname][l], stg[:64, 0:128], [sr], [])
        GB, GBr = T["M2"], T["M2_r"]
        for j in range(NCC):
            self.act(GB[:, j, :ncol], SA[:, j, :ncol], AF.Copy, [SAr[j]], [GBr[j]])
        SO, SOr = SA, SAr
        for og in range(0, NCC, 4):
            wt, wr = self.wget_tile(A["w_glu"][l], 0, 16, og * 128, 512)
            for o4 in range(4):
                o = og + o4
                ps = self.proj_group(wt, wr, o4, 16, GB, GBr, ncol)
                sg, sgr = self.rot("sq")
                self.act(sg[:, :ncol], ps[0][:, :ncol], AF.Sigmoid, [ps[1]], [sgr])
                self.dve(lambda e, o=o, sg=sg: e.tensor_tensor(out=SO[:, o, :ncol], in0=SA[:, o, :ncol], in1=sg[:, :ncol], op=ALU.mult),
                         [SAr[o], sgr], [SOr[o]])
        self.norm_to(T["M2"], T["M2_r"], SO, SOr, NCC, ncol, T["v2"][:, (V_BGS + l) * NCC:(V_BGS + l + 1) * NCC], CC)

    def conv_branch(self, A, l, kind, t0, ncol):
        T = self.T
        hT, hr = T["hT"], T["hT_r"]
        SA, SAr = T["SA"], T["SA_r"]
        cw = T["convw"]
        v2 = T["v2"]
        if kind == "s":
            GS, GSr = T["GS"], T["GS_r"]
        for ig in range(0, NCC, 2):
            wv, wvr = self.wget_tile(A["w_in"][l], 0, 32, ig * 128, 256)
            pvs = [self.proj_group(wv, wvr, i2, 32, hT, hr, ncol) for i2 in range(2)]
            wg, wgr = self.wget_tile(A["w_in"][l], 0, 32, CC + ig * 128, 256)
            for i2 in range(2):
                i = ig + i2
                pv = pvs[i2]
                pg = self.proj_group(wg, wgr, i2, 32, hT, hr, ncol)
                sg, sgr = self.rot("sq")
                self.act(sg[:, :ncol], pg[0][:, :ncol], AF.Sigmoid, [pg[1]], [sgr])
                wbase = (l * NCC + i) * 31
                bias = v2[:, (V_CB + l) * NCC + i:(V_CB + l) * NCC + i + 1]
                if kind == "p":
                    gx, gxr = self.rot("GX")
                    self.act(gx[:, 0:30], T["hist"][l][:, i, :], AF.Copy, [T["hist_r"][l][i]], [gxr])
                    self.dve(lambda e, gx=gx, pv=pv, sg=sg: e.tensor_tensor(out=gx[:, 30:30 + ncol], in0=pv[0][:, :ncol],
                                                                            in1=sg[:, :ncol], op=ALU.mult),
                             [pv[1], sgr], [gxr])
                    self.act(T["hist"][l][:, i, :], gx[:, ncol:ncol + 30], AF.Copy, [gxr], [T["hist_r"][l][i]])
                    co = SA[:, i, :ncol]
                    self.dve(lambda e, gx=gx, co=co, wbase=wbase, bias=bias: e.tensor_scalar(
                        out=co, in0=gx[:, 0:ncol], scalar1=cw[:, wbase:wbase + 1], scalar2=bias, op0=ALU.mult, op1=ALU.add),
                        [gxr, T["const_r"]], [SAr[i]])
                    for k in range(1, 31):
                        self.dve(lambda e, gx=gx, co=co, k=k, wbase=wbase: e.scalar_tensor_tensor(
                            out=co, in0=gx[:, k:k + ncol], scalar=cw[:, wbase + k:wbase + k + 1], in1=co, op0=ALU.mult, op1=ALU.add),
                            [gxr, SAr[i], T["const_r"]], [SAr[i]])
                else:
                    self.dve(lambda e, i=i, pv=pv, sg=sg: e.tensor_tensor(out=GS[:, i, :], in0=pv[0][:, :ncol], in1=sg[:, :ncol], op=ALU.mult),
                             [pv[1], sgr], [GSr[i]])
                    self.dve(lambda e, i=i, wbase=wbase, bias=bias: e.tensor_scalar(
                        out=SA[:, i, :ncol], in0=GS[:, i, :], scalar1=cw[:, wbase + 30:wbase + 31], scalar2=bias, op0=ALU.mult, op1=ALU.add),
                        [GSr[i], T["const_r"]], [SAr[i]])
        if kind == "s":
            for rb in range(4):
                rows = A["cconv"][l, rb * 4:(rb + 1) * 4, :, :].rearrange("b k c -> (b k) c")
                for i in range(NCC):
                    if i % 4 == 0:
                        stg, sr = self.rot("stg")
                        self.dma(stg[:120, 0:512], rows[:, i * 128:(i + 4) * 128], [], [sr])
                    pt, pr = self.psum()
                    self.tr(pt[:, 0:120], stg[:120, (i % 4) * 128:(i % 4 + 1) * 128], T["ident"][:120, :120], [sr, T["const_r"]], [pr])
                    wbase = (l * NCC + i) * 31
                    tm, tmr = self.rot("sq")
                    wbc = cw[:, wbase:wbase + 30].unsqueeze(1).broadcast_to([128, 4, 30])
                    self.dve(lambda e, tm=tm, pt=pt, wbc=wbc: e.tensor_tensor(
                        out=tm[:, 0:120].rearrange("p (b k) -> p b k", b=4), in0=pt[:, 0:120].rearrange("p (b k) -> p b k", b=4),
                        in1=wbc, op=ALU.mult), [pr, T["const_r"]], [tmr])
                    self.dve(lambda e, tm=tm: e.tensor_reduce(out=tm[:, 128:132], in_=tm[:, 0:120].rearrange("p (b k) -> p b k", b=4),
                                                              axis=AX.X, op=ALU.add), [tmr], [tmr])
                    self.dve(lambda e, tm=tm, i=i, rb=rb: e.tensor_tensor(out=SA[:, i, rb * 4:rb * 4 + 4], in0=SA[:, i, rb * 4:rb * 4 + 4],
                                                                          in1=tm[:, 128:132], op=ALU.add), [tmr, SAr[i]], [SAr[i]])
            self.dma(A["cv_s"][l, :, 0:29, :], A["cconv"][l, :, 1:30, :], [], [])
            self.store_T(A["cv_s"][l, :, 29, :], GS, GSr, 0, NS, NCC)
        elif t0 + ncol == SEQ:
            self.store_T(A["cv_p"][l], T["hist"][l], T["hist_r"][l], 0, 30, NCC)
        ps, pr = self.psum()
        for i in range(NCC):
            self.mm(ps[:, :ncol], T["ones"][:, :], SA[:, i, :ncol], i == 0, i == NCC - 1, [SAr[i], T["const_r"]], [pr], i == NCC - 1)
        mean, meanr = T["mean"], T["mean_r"]
        self.act(mean[:, :ncol], ps[:, :ncol], AF.Copy, [pr], [meanr], scale=1.0 / CC)
        rb_, rbr = T["rstd2"], T["rstd2_r"]
        self.rstd_bc(SA, SAr, NCC, ncol, rb_, rbr, CC, sub_mean=(mean, meanr))
        for i in range(NCC):
            self.dve(lambda e, i=i: e.tensor_tensor(out=SA[:, i, :ncol], in0=SA[:, i, :ncol], in1=mean[:, :ncol], op=ALU.subtract),
                     [SAr[i], meanr], [SAr[i]])
            self.dve(lambda e, i=i: e.tensor_tensor(out=SA[:, i, :ncol], in0=SA[:, i, :ncol], in1=rb_[:, :ncol], op=ALU.mult),
                     [SAr[i], rbr], [SAr[i]])
            lg = v2[:, (V_LNG + l) * NCC + i:(V_LNG + l) * NCC + i + 1]
            lb = v2[:, (V_LNB + l) * NCC + i:(V_LNB + l) * NCC + i + 1]
            self.act(SA[:, i, :ncol], SA[:, i, :ncol], AF.Silu, [SAr[i], T["const_r"]], [SAr[i]], bias=lb, scale=lg)
        self.norm_to(T["M1"], T["M1_r"], SA, SAr, NCC, ncol, v2[:, (V_BGC + l) * NCC:(V_BGC + l + 1) * NCC], CC)

    def xattn(self, A, l, kind, t0, ncol):
        T = self.T
        xT, xr = T["xT"], T["xT_r"]
        hT, hr = T["hT"], T["hT_r"]
        self.norm_to(hT, hr, xT, xr, ND, ncol, T["v4"][:, (V_X + l) * ND:(V_X + l + 1) * ND], D)
        sc_ = 1.0 / math.sqrt(128.0)
        oT, oTr = T["oT"], T["oT_r"]
        if kind == "p":
            qT, qTr = T["qT"], T["qT_r"]
        else:
            qT, qTr = T["qTf"], T["qTf_r"]
        pss = self.proj4(A["w_xq"][l], 0, [(hT, hr, 0), (hT, hr, 16)], ncol)
        for h in range(4):
            self.act(qT[:, h, :ncol], pss[h][0][:, :ncol], AF.Copy, [pss[h][1]], [qTr[h]], scale=sc_)
        if kind == "p":
            for tb in range(0, ncol, 128):
                for h in range(4):
                    sp_, spr = self.psum()
                    self.mm(sp_[:, :NMEM], qT[:, h, tb:tb + 128], T["KT"][l][:, h, :], True, True, [qTr[h], T["KT_r"][l]], [spr], True)
                    st, str_ = self.rot("st")
                    self.dve(lambda e, st=st, sp_=sp_: e.reduce_max(out=st[:, 0:1], in_=sp_[:, :NMEM], axis=AX.X), [spr], [str_])
                    self.dve(lambda e, st=st: e.tensor_scalar(out=st[:, 1:2], in0=st[:, 0:1], scalar1=-1.0, scalar2=None, op0=ALU.mult),
                             [str_], [str_])
                    ex, exr = self.rot("sq")
                    self.act(ex[:, :NMEM], sp_[:, :NMEM], AF.Exp, [spr, str_], [exr], bias=st[:, 1:2])
                    self.dve(lambda e, st=st, ex=ex: e.tensor_reduce(out=st[:, 2:3], in_=ex[:, :NMEM], axis=AX.X, op=ALU.add), [exr, str_], [str_])
                    self.dve(lambda e, st=st: e.reciprocal(out=st[:, 3:4], in_=st[:, 2:3]), [str_], [str_])
                    self.dve(lambda e, ex=ex, st=st: e.tensor_scalar(out=ex[:, :NMEM], in0=ex[:, :NMEM], scalar1=st[:, 3:4], scalar2=None,
                                                                    op0=ALU.mult), [exr, str_], [exr])
                    wT, wTr = self.rot("wT")
                    pb, pbr = self.psum()
                    for mb in range(2):
                        self.tr(pb[:, mb * 128:(mb + 1) * 128], ex[:, mb * 128:(mb + 1) * 128], T["ident"][:, :], [exr, T["const_r"]], [pbr])
                    self.act(wT[:, :, :], pb[:, 0:256].rearrange("p (a b) -> p a b", a=2), AF.Copy, [pbr], [wTr])
                    op_, opr = self.psum()
                    for mb in range(2):
                        self.mm(op_[:, :128], T["Vb"][l][:, mb, h * 128:(h + 1) * 128], wT[:, mb, :], mb == 0, mb == 1,
                                [T["Vb_r"][l], wTr], [opr], mb == 1)
                    self.act(oT[:, h, tb:tb + 128], op_[:, :128], AF.Copy, [opr], [oTr[h]])
        else:
            for b in range(NS):
                kb, kbr = self.rot("kv")
                self.dma(kb[:, :, :], A["ck"][l, b].rearrange("(a p) n -> p a n", p=128), [], [kbr])
                sps = [self.psum(), self.psum()]
                for h in range(4):
                    pt, pr = self.psum()
                    for mb in range(2):
                        self.tr(pt[:, mb * 128:(mb + 1) * 128], kb[:, mb, h * 128:(h + 1) * 128], T["ident"][:, :], [kbr, T["const_r"]], [pr])
                    kth, ktr = self.rot("kth")
                    self.act(kth[:, :], pt[:, 0:256], AF.Copy, [pr], [ktr])
                    self.mm(sps[h // 2][0][0:1, (h % 2) * 256:(h % 2 + 1) * 256], qT[:, h, b:b + 1], kth[:, :], True, True,
                            [qTr[h], ktr], [sps[h // 2][1]], True)
                vb_, vbr = self.rot("kv")
                self.dma(vb_[:, :, :], A["cvv"][l, b].rearrange("(a p) n -> p a n", p=128), [], [vbr])
                s1, s1r = self.rot("s1")
                sv = s1[0:1, 0:1024].rearrange("p (h m) -> p h m", h=4)
                for hp in range(2):
                    spt, spr = sps[hp]
                    pin = spt[0:1, 0:512].rearrange("p (h m) -> p h m", h=2)
                    mx = s1[0:1, 1024 + 2 * hp:1026 + 2 * hp]
                    svh = s1[0:1, hp * 512:(hp + 1) * 512].rearrange("p (h m) -> p h m", h=2)
                    self.dve(lambda e, mx=mx, pin=pin: e.tensor_reduce(out=mx, in_=pin, axis=AX.X, op=ALU.max), [spr], [s1r])
                    self.dve(lambda e, svh=svh, pin=pin, mx=mx: e.tensor_tensor(
                        out=svh, in0=pin, in1=mx.unsqueeze(2).broadcast_to([1, 2, 256]), op=ALU.subtract), [spr, s1r], [s1r])
                self.act(s1[0:1, 0:1024], s1[0:1, 0:1024], AF.Exp, [s1r], [s1r])
                self.dve(lambda e, s1=s1, sv=sv: e.tensor_reduce(out=s1[0:1, 1028:1032], in_=sv, axis=AX.X, op=ALU.add), [s1r], [s1r])
                self.dve(lambda e, s1=s1: e.reciprocal(out=s1[0:1, 1028:1032], in_=s1[0:1, 1028:1032]), [s1r], [s1r])
                self.dve(lambda e, s1=s1, sv=sv: e.tensor_tensor(out=sv, in0=sv, in1=s1[0:1, 1028:1032].unsqueeze(2).broadcast_to([1, 4, 256]),
                                                                 op=ALU.mult), [s1r], [s1r])
                pt, pr = self.psum()
                for c in range(8):
                    self.tr(pt[:, c:c + 1], s1[0:1, c * 128:(c + 1) * 128], T["ident"][0:1, 0:1], [s1r, T["const_r"]], [pr])
                wc, wcr = self.rot("st")
                self.act(wc[:, 0:8], pt[:, 0:8], AF.Copy, [pr], [wcr])
                op_, opr = self.psum()
                for h in range(4):
                    for mb in range(2):
                        self.mm(op_[:, h:h + 1], vb_[:, mb, h * 128:(h + 1) * 128], wc[:, h * 2 + mb:h * 2 + mb + 1], mb == 0, mb == 1,
                                [vbr, wcr], [opr], mb == 1)
                for h in range(4):
                    self.act(oT[:, h, b:b + 1], op_[:, h:h + 1], AF.Copy, [opr], [oTr[h]])
        for dg in range(0, ND, 16):
            wt, wr = self.wget_tile(A["w_xo"][l], 0, 4, dg * 128, 2048)
            for d16 in range(16):
                d = dg + d16
                ps = self.proj_group(wt, wr, d16, 4, oT, oTr, ncol)
                self.dve(lambda e, d=d, ps=ps: e.tensor_tensor(out=xT[:, d, :ncol], in0=ps[0][:, :ncol], in1=xT[:, d, :ncol], op=ALU.add),
                         [ps[1], xr[d]], [xr[d]])

    def mlp(self, A, l, ncol):
        T = self.T
        xT, xr = T["xT"], T["xT_r"]
        hT, hr = T["hT"], T["hT_r"]
        self.norm_to(hT, hr, xT, xr, ND, ncol, T["v4"][:, (V_FFN + l) * ND:(V_FFN + l + 1) * ND], D)
        aT, aTr = T["M1"], T["M1_r"]
        banks = [(T["ps"][i], T["ps_r"][i]) for i in range(6)] + [(T["psy"][0], T["psy_r"][0]), (T["psbf"], T["psb_r"][0])]
        FB = 8
        for fb in range(0, FF // 128, FB):
            for kt in range(4):
                wt, wr = self.wget_tile(A["w_up"][l], kt * 1024, 8, fb * 128, 1024)
                for o in range(FB):
                    pt, pr = banks[o]
                    for k in range(8):
                        c = kt * 8 + k
                        self.mm(pt[:, :ncol], wt[:, k, o * 128:(o + 1) * 128], hT[:, c, :ncol], c == 0, c == 31,
                                [wr, hr[c]], [pr], True)
            for o in range(FB):
                pt, pr = banks[o]
                rl, rlr = self.rot("sq")
                self.act(rl[:, :ncol], pt[:, :ncol], AF.Relu, [pr], [rlr])
                self.dve(lambda e, o=o, rl=rl: e.tensor_tensor(out=aT[:, o, :ncol], in0=rl[:, :ncol], in1=rl[:, :ncol], op=ALU.mult),
                         [rlr], [aTr[o]])
            for dg in range(0, ND, 8):
                wt, wr = self.wget_tile(A["w_down"][l], fb * 128, FB, dg * 128, 1024)
                for d8 in range(8):
                    d = dg + d8
                    ps = self.proj_group(wt, wr, d8, FB, aT, aTr, ncol)
                    self.dve(lambda e, d=d, ps=ps: e.tensor_tensor(out=xT[:, d, :ncol], in0=ps[0][:, :ncol], in1=xT[:, d, :ncol], op=ALU.add),
                             [ps[1], xr[d]], [xr[d]])

    def final_out(self, A, kind, t0, ncol):
        T = self.T
        xT, xr = T["xT"], T["xT_r"]
        rb, rr = T["rstd"], T["rstd_r"]
        self.rstd_bc(xT, xr, ND, ncol, rb, rr, D)
        g = T["v4"][:, V_FINAL * ND:(V_FINAL + 1) * ND]
        for c in range(ND):
            self.dve(lambda e, c=c: e.scalar_tensor_tensor(out=xT[:, c, :ncol], in0=xT[:, c, :ncol], scalar=g[:, c:c + 1], in1=rb[:, :ncol],
                                                           op0=ALU.mult, op1=ALU.mult), [xr[c], rr, T["const_r"]], [xr[c]])
        if kind == "p":
            for r0 in range(0, ncol, 128):
                self.store_T(A["y_p"][t0 + r0:t0 + r0 + 128, :], xT, xr, r0, 128, ND)
        else:
            self.store_T(A["y_s"][:, :], xT, xr, 0, NS, ND)


def build_program():
    nc = bass.Bass("TRN2", target_bir_lowering=False)

    def din(name, shape):
        return nc.dram_tensor(name, list(shape), F32, kind="ExternalInput").ap()

    def dout(name, shape):
        return nc.dram_tensor(name, list(shape), F32, kind="ExternalOutput").ap()

    A = {}
    A["xp"] = din("xp", [SEQ, D])
    A["xs"] = din("xs", [NS, D])
    A["mem"] = din("mem", [NMEM, D])
    A["cconv"] = din("cconv", [L, NS, 30, CC])
    A["st_re"] = din("st_re", [L, NS, 8192])
    A["st_im"] = din("st_im", [L, NS, 8192])
    A["ck"] = din("ck", [L, NS, NMEM, 512])
    A["cvv"] = din("cvv", [L, NS, NMEM, 512])
    A["ident"] = din("ident", [128, 128])
    A["vecs4096"] = din("vecs4096", [128, 9 * ND])
    A["vecs2048"] = din("vecs2048", [128, 12 * NCC])
    A["convw"] = din("convw", [128, L * NCC * 31])
    A["a_re"] = din("a_re", [128, L, 64])
    A["a_im"] = din("a_im", [128, L, 64])
    A["log_dt"] = din("log_dt", [128, L, 64])
    A["b_re"] = din("b_re", [128, L, 64, 32])
    A["b_im"] = din("b_im", [128, L, 64, 32])
    A["c_re"] = din("c_re", [128, L, 64, 32])
    A["c_im"] = din("c_im", [128, L, 64, 32])
    A["w_in"] = din("w_in", [L, D, 3 * CC])
    A["w_glu"] = din("w_glu", [L, CC, CC])
    A["w_out"] = din("w_out", [L, D, D])
    A["w_xq"] = din("w_xq", [L, D, 512])
    A["w_xk"] = din("w_xk", [L, D, 512])
    A["w_xv"] = din("w_xv", [L, D, 512])
    A["w_xo"] = din("w_xo", [L, 512, D])
    A["w_up"] = din("w_up", [L, D, FF])
    A["w_down"] = din("w_down", [L, FF, D])
    A["y_p"] = dout("y_p", [SEQ, D])
    A["y_s"] = dout("y_s", [NS, D])
    A["mk"] = dout("mk", [L, NMEM, 512])
    A["mv"] = dout("mv", [L, NMEM, 512])
    A["cv_p"] = dout("cv_p", [L, 30, CC])
    A["sr_p"] = dout("sr_p", [L, 64, 128])
    A["si_p"] = dout("si_p", [L, 64, 128])
    A["cv_s"] = dout("cv_s", [L, NS, 30, CC])
    A["sr_s"] = dout("sr_s", [L, NS, 8192])
    A["si_s"] = dout("si_s", [L, NS, 8192])

    es = ExitStack()
    T = {}

    def sb(name, shape, dt=F32):
        return es.enter_context(nc.sbuf_tensor("sb_" + name, list(shape), dt))

    def add(name, shape, dt=F32, nres=None):
        T[name] = sb(name, shape, dt)
        T[name + "_r"] = Res() if nres is None else RL(nres)

    def addn(name, n, shape, dt=F32):
        T[name] = [sb("%s%d" % (name, i), shape, dt) for i in range(n)]
        T[name + "_r"] = RL(n)

    T["const_r"] = Res()
    T["ones"] = sb("ones", [128, 128])
    T["eps"] = sb("eps", [128, 1])
    T["negpi"] = sb("negpi", [128, 1])
    T["ident"] = sb("ident", [128, 128])
    T["v4"] = sb("v4", [128, 9 * ND])
    T["v2"] = sb("v2", [128, 12 * NCC])
    T["convw"] = sb("convw", [128, L * NCC * 31])
    add("xT", [128, ND, TT], F32, ND)
    add("hT", [128, ND, TT], BF16, ND)
    add("M1", [128, NCC, TT], BF16, NCC)
    T["M2"] = T["hT"][:, 0:NCC, :]
    T["M2_r"] = T["hT_r"][0:NCC]
    add("SA", [128, NCC, TT], F32, NCC)
    add("rstd", [128, TT])
    add("rstd2", [128, TT])
    add("mean", [128, TT])
    add("qT", [128, 4, TT], BF16, 4)
    add("qTf", [128, 4, NS], F32, 4)
    add("oT", [128, 4, TT], BF16, 4)
    add("GS", [128, NCC, NS], F32, NCC)
    T["KT"] = [sb("KT%d" % l, [128, 4, NMEM], BF16) for l in range(L)]
    T["KT_r"] = RL(L)
    T["Vb"] = [sb("Vb%d" % l, [128, 2, 512], BF16) for l in range(L)]
    T["Vb_r"] = RL(L)
    T["hist"] = [sb("hist%d" % l, [128, NCC, 30]) for l in range(L)]
    T["hist_r"] = [RL(NCC) for l in range(L)]
    T["Sre"] = [sb("Sre%d" % l, [128, 64]) for l in range(L)]
    T["Sim"] = [sb("Sim%d" % l, [128, 64]) for l in range(L)]
    T["S_r"] = RL(L)
    T["sc"] = {k: sb("sc_" + k, [128, 64]) for k in ("are", "aim", "dt", "t0", "t1", "mag", "ang", "sin", "cos", "nr", "den", "cr", "ci")}
    T["sc_r"] = Res()
    T["sc_i"] = sb("sc_i", [128, 64], mybir.dt.int32)
    T["Apow"] = sb("Apow", [128, 64, 9, 3])
    for nm in ("tB_re", "tB_im", "tC_re", "tC_im"):
        addn(nm, 4, [128, 128])
    addn("wslot", NSLOT, [128, SLOT_ELEMS], BF16)
    addn("stg", 2, [128, 512])
    addn("sq", 3, [128, 512])
    addn("bc", 4, [128, 128])
    addn("cc", 2, [128, 128])
    addn("hss", 2, [128, 2, NS])
    addn("U", 4, [128, TT])
    addn("HS", 2, [128, 4, TT + 1])
    T["SsR"] = T["HS"][0][:, 0:4, 0:256].rearrange("p a (b c) -> p a b c", c=NS)
    T["SsI"] = T["HS"][1][:, 0:4, 0:256].rearrange("p a (b c) -> p a b c", c=NS)
    T["SsR_r"] = [T["HS_r"][0]] * 64
    T["SsI_r"] = [T["HS_r"][1]] * 64
    addn("GX", 2, [128, 30 + TT])
    addn("st", 4, [128, 8])
    addn("wT", 2, [128, 2, 128], BF16)
    addn("kv", 1, [128, 2, 512])
    addn("kth", 2, [128, 256])
    addn("s1", 1, [1, 1040])
    T["ps"] = [es.enter_context(nc.psum_tensor("ps%d" % i, [128, 512], F32)) for i in range(6)]
    T["ps_r"] = RL(6)
    T["psy"] = [es.enter_context(nc.psum_tensor("psy0", [128, 512], F32))]
    T["psy_r"] = RL(1)
    T["psbf"] = es.enter_context(nc.psum_tensor("psb0", [128, 512], F32))
    T["psb_r"] = RL(1)

    dry = Builder(nc, Prog(nc, True), T, True, None)
    dry.build(A)
    wlist = dry.wrec
    def reset(o):
        if isinstance(o, Res):
            o.w = None
            o.r = {}
        elif isinstance(o, (list, tuple)):
            for x in o:
                reset(x)
    for k, v in T.items():
        if k.endswith("_r"):
            reset(v)
    prog = Prog(nc, False)
    es.enter_context(prog.es)
    real = Builder(nc, prog, T, False, wlist)
    real.build(A)
    with nc.Block() as block:
        @block.tensor
        def _(e):
            for f in prog.q["pe"]:
                f(e)

        @block.scalar
        def _(e):
            for f in prog.q["act"]:
                f(e)

        @block.vector
        def _(e):
            for f in prog.q["dve"]:
                f(e)

        @block.gpsimd
        def _(e):
            for f in prog.q["pool"]:
                f(e)

        @block.sync
        def _(e):
            for f in prog.q["sp"]:
                f(e)
    es.close()
    return nc, prog


def _fm(v, n):
    return np.ascontiguousarray(v.reshape(n, 128).T)


def kernel(**inp):
    f = lambda k: np.asarray(inp[k], dtype=np.float32)
    x_prompt, x_sample, mem_prompt = f("x_prompt"), f("x_sample"), f("mem_prompt")
    cache_conv, st_re, st_im = f("cache_conv"), f("state_ssm_re"), f("state_ssm_im")
    ck, cv = f("cache_mem_k"), f("cache_mem_v")
    v4 = []
    for nm in ("norm_mix_g", "norm_x_g", "norm_mem_g", "norm_ffn_g"):
        for l in range(L):
            v4.append(_fm(f(nm)[l], ND))
    v4.append(_fm(f("norm_final_g"), ND))
    vecs4096 = np.ascontiguousarray(np.concatenate(v4, axis=1))
    v2 = []
    for nm in ("conv_b", "conv_ln_g", "conv_ln_b", "ssm_d", "branch_g_conv", "branch_g_ssm"):
        for l in range(L):
            v2.append(_fm(f(nm)[l], NCC))
    vecs2048 = np.ascontiguousarray(np.concatenate(v2, axis=1))
    cw = f("conv_w")
    convw = np.ascontiguousarray(cw.reshape(L, 31, NCC, 128).transpose(3, 0, 2, 1).reshape(128, L * NCC * 31))

    def gn(a):
        return np.ascontiguousarray(a.reshape(L, 64, 2, 64).transpose(2, 3, 0, 1).reshape(128, L, 64))
    a_re, a_im = gn(f("ssm_a_re")), gn(f("ssm_a_im"))
    ldt = f("ssm_log_dt")
    log_dt = gn(np.broadcast_to(ldt[:, :, None], (L, 128, 64)))

    def bd_b(b):
        o = np.zeros((2, 64, L, 64, 2, 16), np.float32)
        bb = b.reshape(L, 64, 2, 64, 16)
        for g2 in range(2):
            o[g2, :, :, :, g2, :] = bb[:, :, g2, :, :].transpose(2, 0, 1, 3)
        return np.ascontiguousarray(o.reshape(128, L, 64, 32))

    def bd_c(c):
        o = np.zeros((2, 64, L, 64, 2, 16), np.float32)
        cc = c.reshape(L, 64, 2, 16, 64)
        for g2 in range(2):
            o[g2, :, :, :, g2, :] = cc[:, :, g2, :, :].transpose(3, 0, 1, 2)
        return np.ascontiguousarray(o.reshape(128, L, 64, 32))
    b_re, b_im = bd_b(f("ssm_b_re")), bd_b(f("ssm_b_im"))
    c_re, c_im = bd_c(f("ssm_c_re")), bd_c(f("ssm_c_im"))
    shared = dict(ident=np.eye(128, dtype=np.float32), vecs4096=vecs4096, vecs2048=vecs2048, convw=convw,
                  a_re=a_re, a_im=a_im, log_dt=log_dt, b_re=b_re, b_im=b_im, c_re=c_re, c_im=c_im,
                  w_in=f("w_in"), w_glu=f("w_glu"), w_out=f("w_out"), w_xq=f("w_xq"), w_xk=f("w_xk"),
                  w_xv=f("w_xv"), w_xo=f("w_xo"), w_up=f("w_up"), w_down=f("w_down"))
    in_maps = []
    for c in range(8):
        b = c % 4
        s0 = c * NS
        m = dict(shared)
        m["xp"] = np.ascontiguousarray(x_prompt[b])
        m["xs"] = np.ascontiguousarray(x_sample[s0:s0 + NS, 0])
        m["mem"] = np.ascontiguousarray(mem_prompt[b])
        m["cconv"] = np.ascontiguousarray(cache_conv[:, s0:s0 + NS])
        m["st_re"] = np.ascontiguousarray(st_re[:, s0:s0 + NS].reshape(L, NS, 8192))
        m["st_im"] = np.ascontiguousarray(st_im[:, s0:s0 + NS].reshape(L, NS, 8192))
        m["ck"] = np.ascontiguousarray(ck[:, s0:s0 + NS].reshape(L, NS, NMEM, 512))
        m["cvv"] = np.ascontiguousarray(cv[:, s0:s0 + NS].reshape(L, NS, NMEM, 512))
        in_maps.append(m)
    nc, prog = build_program()
    res = run_bass_kernel_spmd(nc, in_maps, core_ids=list(range(8)))
    R = res.results
    y_prompt = np.stack([R[b]["y_p"] for b in range(4)]).astype(np.float32)
    y_sample = np.concatenate([R[c]["y_s"] for c in range(8)], axis=0).reshape(128, 1, D).astype(np.float32)
    mk = np.stack([R[b]["mk"] for b in range(4)], axis=1).reshape(L, 4, NMEM, 4, 128)
    mv = np.stack([R[b]["mv"] for b in range(4)], axis=1).reshape(L, 4, NMEM, 4, 128)
    cvp = np.stack([R[b]["cv_p"] for b in range(4)], axis=1)
    srp = np.stack([R[b]["sr_p"] for b in range(4)], axis=1).reshape(L, 4, 128, 64)
    sip = np.stack([R[b]["si_p"] for b in range(4)], axis=1).reshape(L, 4, 128, 64)
    cvs = np.concatenate([R[c]["cv_s"] for c in range(8)], axis=1)
    srs = np.concatenate([R[c]["sr_s"] for c in range(8)], axis=1).reshape(L, 128, 128, 64)
    sis = np.concatenate([R[c]["si_s"] for c in range(8)], axis=1).reshape(L, 128, 128, 64)
    return (y_prompt, y_sample, mk.astype(np.float32), mv.astype(np.float32), cvp.astype(np.float32),
            srp.astype(np.float32), sip.astype(np.float32), cvs.astype(np.float32), srs.astype(np.float32),
            sis.astype(np.float32))
```

```python
import math
from contextlib import ExitStack
import numpy as np
import concourse.bass as bass
import concourse.mybir as mybir
from concourse.bass_utils import run_bass_kernel_spmd

F32 = mybir.dt.float32
BF16 = mybir.dt.bfloat16
AF = mybir.ActivationFunctionType
ALU = mybir.AluOpType
AX = mybir.AxisListType

D = 4096
ND = 32
CC = 2048
NCC = 16
FF = 16384
L = 2
SEQ = 2048
TT = 256
NS = 16
NMEM = 256
EPS = 1e-6
NSLOT = 3
SLOT_ELEMS = 8192
NDMA = 24
SEM_LIMIT = 20000
PI = math.pi

V_MIX, V_X, V_MEM, V_FFN, V_FINAL = 0, 2, 4, 6, 8
V_CB, V_LNG, V_LNB, V_SD, V_BGC, V_BGS = 0, 2, 4, 6, 8, 10


class Res:
    __slots__ = ("w", "r")

    def __init__(self):
        self.w = None
        self.r = {}


def RL(n):
    return [Res() for _ in range(n)]


class Prog:
    ENG = ["pe", "act", "dve", "pool", "sp"]

    def __init__(self, nc, dry):
        self.nc = nc
        self.dry = dry
        self.q = {e: [] for e in self.ENG}
        self.absq = {e: [] for e in self.ENG}
        self.nsem = 0
        self.es = ExitStack()
        self.pe_sems = set()
        self.cur = {e: [self.new_sem(), 0] for e in ["pe", "act", "dve", "pool"]}
        self.pe_sems.add(id(self.cur["pe"][0]))
        self.pending = {e: False for e in ["pe", "act", "dve", "pool"]}
        self.waited = {e: {} for e in self.ENG}
        self.dma_slots = [[self.new_sem(), 0] for _ in range(NDMA)]
        self.dma_i = 0
        self.ninstr = 0

    def new_sem(self):
        self.nsem += 1
        if self.dry:
            return ("sem", self.nsem)
        return self.es.enter_context(self.nc.semaphore("s%d" % self.nsem))

    def issue(self, eng, fn, reads=(), writes=(), dma=False, mark=True):
        waits = []
        wd = self.waited[eng]

        pe_sems = self.pe_sems

        def need(tok):
            if tok is None:
                return
            sem, val = tok
            if eng == "pe" and id(sem) in pe_sems:
                return
            if wd.get(id(sem), 0) >= val:
                return
            wd[id(sem)] = val
            waits.append((sem, val))

        for r in reads:
            need(r.w)
        for w_ in writes:
            need(w_.w)
            for t in w_.r.values():
                need(t)
        if dma:
            slot = self.dma_slots[self.dma_i % NDMA]
            self.dma_i += 1
            if slot[1] > 0:
                need((slot[0], slot[1]))
            slot[1] += 16
            tok = (slot[0], slot[1])
            inc = (slot[0], 16)
        else:
            c = self.cur[eng]
            if mark:
                if c[1] >= SEM_LIMIT and not self.pending[eng]:
                    c = self.cur[eng] = [self.new_sem(), 0]
                    if eng == "pe":
                        self.pe_sems.add(id(c[0]))
                c[1] += 1
                tok = (c[0], c[1])
                inc = (c[0], 1)
                self.pending[eng] = False
            else:
                tok = (c[0], c[1] + 1)
                inc = None
                self.pending[eng] = True
        for r in reads:
            k = id(tok[0])
            if k not in r.r or r.r[k][1] < tok[1]:
                r.r[k] = tok
        for w_ in writes:
            w_.w = tok
            w_.r = {}
        self.ninstr += 1
        self.absq[eng].append(([(id(a), b) for a, b in waits], None if inc is None else (id(inc[0]), inc[1])))
        if not self.dry:
            def thunk(e, waits=waits, fn=fn, inc=inc):
                for sem, val in waits:
                    e.wait_ge(sem, val)
                ins = fn(e)
                if inc is not None:
                    ins.then_inc(inc[0], inc[1])
            self.q[eng].append(thunk)
        return tok

    def finish(self):
        if self.dry:
            return
        fin = [(s[0], s[1]) for s in self.dma_slots if s[1] > 0]

        def thunk(e):
            for sem, val in fin:
                e.wait_ge(sem, val)
        self.q["sp"].append(thunk)


class Builder:
    def __init__(self, nc, prog, T, dry, wlist):
        self.nc = nc
        self.p = prog
        self.T = T
        self.dry = dry
        self.wlist = wlist
        self.wrec = []
        self.wissued = 0
        self.wget = 0
        self.ps_i = 0
        self.sq_i = 0
        self.misc_i = {}

    def psum(self):
        i = self.ps_i % len(self.T["ps"])
        self.ps_i += 1
        return self.T["ps"][i], self.T["ps_r"][i]

    def rot(self, name):
        lst = self.T[name]
        i = self.misc_i.get(name, 0)
        self.misc_i[name] = i + 1
        return lst[i % len(lst)], self.T[name + "_r"][i % len(lst)]

    def wload_issue(self, k):
        src, a, b = self.wlist[k]
        slot = self.T["wslot"][k % NSLOT]
        res = self.T["wslot_r"][k % NSLOT]
        dst = slot[:, 0:a * b].rearrange("p (a n) -> p a n", a=a)
        self.p.issue("pool", lambda e, dst=dst, src=src: e.dma_start(out=dst, in_=src),
                     writes=[res], dma=True)

    def wget_tile(self, w_ap, r0, a, c0, b):
        src = w_ap[r0:r0 + a * 128, c0:c0 + b].rearrange("(a p) n -> p a n", p=128)
        k = self.wget
        self.wget += 1
        if self.dry:
            self.wrec.append((src, a, b))
        else:
            upto = min(k + NSLOT - 1, len(self.wlist) - 1)
            while self.wissued <= upto:
                self.wload_issue(self.wissued)
                self.wissued += 1
        slot = self.T["wslot"][k % NSLOT]
        return slot[:, 0:a * b].rearrange("p (a n) -> p a n", a=a), self.T["wslot_r"][k % NSLOT]

    def act(self, out, in_, func, reads, writes, bias=None, scale=None, accum=None):
        kw = {}
        if accum is not None:
            kw["accum_out"] = accum
        if bias is not None:
            kw["bias"] = bias
        if scale is not None:
            kw["scale"] = scale
        return self.p.issue("act", lambda e: e.activation(out=out, in_=in_, func=func, **kw),
                            reads=reads, writes=writes)

    def dve(self, fn, reads, writes):
        return self.p.issue("dve", fn, reads=reads, writes=writes)

    def mm(self, out, lhsT, rhs, start, stop, reads, writes, mark):
        return self.p.issue("pe", lambda e: e.matmul(out, lhsT, rhs, start=start, stop=stop),
                            reads=reads, writes=writes, mark=True)

    def tr(self, out, in_, ident, reads, writes):
        return self.p.issue("pe", lambda e: e.transpose(out, in_, ident), reads=reads, writes=writes)

    def dma(self, out, in_, reads, writes):
        return self.p.issue("sp", lambda e: e.dma_start(out=out, in_=in_), reads=reads, writes=writes, dma=True)

    def load_T(self, dstT, dst_res, col0, rows_ap, nrows, nchunks):
        T = self.T
        for h0 in range(0, nchunks, 4):
            nh = min(4, nchunks - h0)
            stg, sr = self.rot("stg")
            self.dma(stg[:nrows, 0:nh * 128], rows_ap[:, h0 * 128:(h0 + nh) * 128], [], [sr])
            for g0 in range(0, nh, 4):
                ps, pr = self.psum()
                for c in range(4):
                    self.tr(ps[:, c * 128:c * 128 + nrows], stg[:nrows, (g0 + c) * 128:(g0 + c + 1) * 128],
                            T["ident"][:nrows, :nrows], [sr, T["const_r"]], [pr])
                o = dstT[:, h0 + g0:h0 + g0 + 4, col0:col0 + nrows]
                i = ps[:, 0:512].rearrange("p (c n) -> p c n", c=4)[:, :, 0:nrows]
                self.act(o, i, AF.Copy, [pr], dst_res[h0 + g0:h0 + g0 + 4])

    def store_T(self, rows_ap, srcT, src_res, col0, nrows, nchunks):
        T = self.T
        for h0 in range(0, nchunks, 4):
            nh = min(4, nchunks - h0)
            stg, sr = self.rot("stg")
            for g0 in range(0, nh, 4):
                ps, pr = self.psum()
                for c in range(4):
                    self.tr(ps[:nrows, c * 128:(c + 1) * 128], srcT[:, h0 + g0 + c, col0:col0 + nrows],
                            T["ident"][:, :], [src_res[h0 + g0 + c], T["const_r"]], [pr])
                self.act(stg[:nrows, g0 * 128:(g0 + 4) * 128], ps[:nrows, 0:512], AF.Copy, [pr], [sr])
            self.dma(rows_ap[:, h0 * 128:(h0 + nh) * 128], stg[:nrows, 0:nh * 128], [sr], [])

    def rstd_bc(self, src, src_res, nch, ncol, out, out_r, nfeat, sub_mean=None):
        T = self.T
        ps, pr = self.psum()
        for c in range(nch):
            sq, sr = self.rot("sq")
            self.act(sq[:, :ncol], src[:, c, :ncol], AF.Square, [src_res[c]], [sr])
            self.mm(ps[:, :ncol], T["ones"][:, :], sq[:, :ncol], c == 0, c == nch - 1,
                    [sr, T["const_r"]], [pr], mark=(c == nch - 1))
        if sub_mean is None:
            self.act(out[:, :ncol], ps[:, :ncol], AF.Sqrt, [pr, T["const_r"]], [out_r],
                     bias=T["eps"][:, 0:1], scale=1.0 / nfeat)
        else:
            mb, mr = sub_mean
            v, vr = self.rot("sq")
            self.dve(lambda e: e.tensor_tensor(out=v[:, :ncol], in0=mb[:, :ncol], in1=mb[:, :ncol], op=ALU.mult),
                     [mr], [vr])
            self.dve(lambda e: e.scalar_tensor_tensor(out=v[:, :ncol], in0=ps[:, :ncol], scalar=1.0 / nfeat,
                                                      in1=v[:, :ncol], op0=ALU.mult, op1=ALU.subtract),
                     [pr, vr], [vr])
            self.act(out[:, :ncol], v[:, :ncol], AF.Sqrt, [vr, T["const_r"]], [out_r], bias=T["eps"][:, 0:1], scale=1.0)
        self.dve(lambda e: e.reciprocal(out=out[:, :ncol], in_=out[:, :ncol]), [out_r], [out_r])

    def norm_to(self, dst, dst_res, src, src_res, nch, ncol, gvec, nfeat):
        T = self.T
        rb, rr = T["rstd"], T["rstd_r"]
        self.rstd_bc(src, src_res, nch, ncol, rb, rr, nfeat)
        for c in range(nch):
            self.dve(lambda e, c=c: e.scalar_tensor_tensor(out=dst[:, c, :ncol], in0=src[:, c, :ncol],
                                                           scalar=gvec[:, c:c + 1], in1=rb[:, :ncol],
                                                           op0=ALU.mult, op1=ALU.mult),
                     [src_res[c], rr, T["const_r"]], [dst_res[c]])

    def proj_group(self, w_tile, w_res, col, nk, rhsT, rhs_res, ncol, kofs=0, ps=None, first=True, last=True):
        if ps is None:
            ps = self.psum()
        pt, pr = ps
        for k in range(nk):
            self.mm(pt[:, :ncol], w_tile[:, k, col * 128:(col + 1) * 128], rhsT[:, kofs + k, :ncol],
                    first and k == 0, last and k == nk - 1,
                    [w_res, rhs_res[kofs + k]], [pr], mark=(last and k == nk - 1))
        return ps

    def proj4(self, w_ap, col0, halves, ncol):
        pss = [self.psum() for _ in range(4)]
        for kh in range(2):
            rhsT, rhs_res, kofs = halves[kh]
            wt, wr = self.wget_tile(w_ap, kh * 2048, 16, col0, 512)
            for o in range(4):
                pt, pr = pss[o]
                for k in range(16):
                    self.mm(pt[:, :ncol], wt[:, k, o * 128:(o + 1) * 128], rhsT[:, kofs + k, :ncol],
                            kh == 0 and k == 0, kh == 1 and k == 15, [wr, rhs_res[kofs + k]], [pr], True)
        return pss

    def build(self, A):
        T = self.T
        p = self.p
        self.p.issue("dve", lambda e: e.memset(T["ones"][:, :], 1.0), [], [T["const_r"]])
        self.p.issue("dve", lambda e: e.memset(T["eps"][:, :], EPS), [], [T["const_r"]])
        self.p.issue("dve", lambda e: e.memset(T["negpi"][:, :], -PI), [], [T["const_r"]])
        self.dma(T["ident"][:, :], A["ident"], [], [T["const_r"]])
        self.dma(T["v4"][:, :], A["vecs4096"], [], [T["const_r"]])
        self.dma(T["v2"][:, :], A["vecs2048"], [], [T["const_r"]])
        self.dma(T["convw"][:, :], A["convw"], [], [T["const_r"]])
        for l in range(L):
            for i in range(NCC):
                self.p.issue("dve", lambda e, l=l, i=i: e.memset(T["hist"][l][:, i, :], 0.0), [], [T["hist_r"][l][i]])
            self.p.issue("dve", lambda e, l=l: e.memset(T["Sre"][l][:, :], 0.0), [], [T["S_r"][l]])
            self.p.issue("dve", lambda e, l=l: e.memset(T["Sim"][l][:, :], 0.0), [], [T["S_r"][l]])
        for q4 in range(4):
            for nm in ("tB_re", "tB_im", "tC_re", "tC_im"):
                self.p.issue("dve", lambda e, nm=nm, q4=q4: e.memset(T[nm][q4][:, :], 0.0), [], [T[nm + "_r"][q4]])

        self.mem_kv(A)
        tiles = [("p", t0, TT) for t0 in range(0, SEQ, TT)] + [("s", 0, NS)]
        for (kind, t0, ncol) in tiles:
            xT, xr = T["xT"], T["xT_r"]
            if kind == "p":
                for r0 in range(0, ncol, 128):
                    self.load_T(xT, xr, r0, A["xp"][t0 + r0:t0 + r0 + 128, :], 128, ND)
            else:
                self.load_T(xT, xr, 0, A["xs"][:, :], NS, ND)
            for l in range(L):
                self.layer(A, l, kind, t0, ncol)
            self.final_out(A, kind, t0, ncol)
        p.finish()

    def mem_kv(self, A):
        T = self.T
        xT, xr = T["xT"], T["xT_r"]
        hT, hr = T["hT"], T["hT_r"]
        for r0 in range(0, NMEM, 128):
            self.load_T(xT, xr, r0, A["mem"][r0:r0 + 128, :], 128, ND)
        for l in range(L):
            g = T["v4"][:, (V_MEM + l) * ND:(V_MEM + l + 1) * ND]
            self.norm_to(hT, hr, xT, xr, ND, NMEM, g, D)
            for hh in range(2):
                wt, wr = self.wget_tile(A["w_xk"][l], 0, 32, hh * 256, 256)
                for c2 in range(2):
                    h = hh * 2 + c2
                    ps = self.proj_group(wt, wr, c2, 32, hT, hr, NMEM)
                    self.act(T["KT"][l][:, h, :], ps[0][:, :NMEM], AF.Copy, [ps[1]], [T["KT_r"][l]])
            for which, wname, oname in ((0, "w_xk", "mk"), (1, "w_xv", "mv")):
                pss = [self.psum(), self.psum()]
                for half in range(2):
                    wt, wr = self.wget_tile(A[wname][l], half * 2048, 16, 0, 512)
                    for mb in range(2):
                        pt, pr = pss[mb]
                        for k in range(16):
                            c = half * 16 + k
                            self.mm(pt[:, :512], hT[:, c, mb * 128:(mb + 1) * 128], wt[:, k, :],
                                    c == 0, c == 31, [wr, hr[c]], [pr], mark=(k == 15))
                for mb in range(2):
                    pt, pr = pss[mb]
                    stg, sr = self.rot("stg")
                    self.act(stg[:, 0:512], pt[:, :512], AF.Copy, [pr], [sr])
                    self.dma(A[oname][l, mb * 128:(mb + 1) * 128, :], stg[:, 0:512], [sr], [])
                    if which == 1:
                        self.act(T["Vb"][l][:, mb, :], pt[:, :512], AF.Copy, [pr], [T["Vb_r"][l]])

    def layer(self, A, l, kind, t0, ncol):
        T = self.T
        xT, xr = T["xT"], T["xT_r"]
        hT, hr = T["hT"], T["hT_r"]
        self.norm_to(hT, hr, xT, xr, ND, ncol, T["v4"][:, (V_MIX + l) * ND:(V_MIX + l + 1) * ND], D)
        self.conv_branch(A, l, kind, t0, ncol)
        self.ssm_consts(A, l)
        self.ssm_branch(A, l, kind, t0, ncol)
        M1, M1r, M2, M2r = T["M1"], T["M1_r"], T["M2"], T["M2_r"]
        for dg in range(0, ND, 4):
            pss = self.proj4(A["w_out"][l], dg * 128, [(M1, M1r, 0), (M2, M2r, 0)], ncol)
            for d4 in range(4):
                d = dg + d4
                pt, pr = pss[d4]
                self.dve(lambda e, d=d, pt=pt: e.tensor_tensor(out=xT[:, d, :ncol], in0=pt[:, :ncol], in1=xT[:, d, :ncol],
                                                              op=ALU.add), [pr, xr[d]], [xr[d]])
        self.xattn(A, l, kind, t0, ncol)
        self.mlp(A, l, ncol)

    def ssm_consts(self, A, l):
        T = self.T
        cr_ = T["sc_r"]
        sc = T["sc"]
        self.dma(sc["are"][:, :], A["a_re"][:, l, :], [], [cr_])
        self.dma(sc["aim"][:, :], A["a_im"][:, l, :], [], [cr_])
        self.dma(sc["dt"][:, :], A["log_dt"][:, l, :], [], [cr_])
        R = [cr_]
        a = lambda out, in_, func, **kw: self.act(out, in_, func, R, R, **kw)
        v = lambda fn: self.dve(fn, R, R)
        a(sc["dt"][:, :], sc["dt"][:, :], AF.Exp)
        v(lambda e: e.tensor_tensor(out=sc["t0"][:, :], in0=sc["are"][:, :], in1=sc["dt"][:, :], op=ALU.mult))
        a(sc["mag"][:, :], sc["t0"][:, :], AF.Exp)
        v(lambda e: e.tensor_tensor(out=sc["ang"][:, :], in0=sc["aim"][:, :], in1=sc["dt"][:, :], op=ALU.mult))
        ki = T["sc_i"]

        def reduce_to(dst, src_fn):
            src_fn(sc["t1"])
            v(lambda e: e.tensor_scalar(out=sc["t0"][:, :], in0=sc["t1"][:, :], scalar1=1.0 / (2 * PI), scalar2=None, op0=ALU.mult))
            v(lambda e: e.tensor_copy(out=ki[:, :], in_=sc["t0"][:, :]))
            v(lambda e: e.tensor_copy(out=sc["t0"][:, :], in_=ki[:, :]))
            v(lambda e: e.scalar_tensor_tensor(out=dst[:, :], in0=sc["t0"][:, :], scalar=-2 * PI, in1=sc["t1"][:, :],
                                               op0=ALU.mult, op1=ALU.add))
            v(lambda e: e.tensor_scalar(out=sc["t0"][:, :], in0=dst[:, :], scalar1=PI, scalar2=None, op0=ALU.is_gt))
            v(lambda e: e.scalar_tensor_tensor(out=dst[:, :], in0=sc["t0"][:, :], scalar=-2 * PI, in1=dst[:, :],
                                               op0=ALU.mult, op1=ALU.add))
            v(lambda e: e.tensor_scalar(out=sc["t0"][:, :], in0=dst[:, :], scalar1=-PI, scalar2=None, op0=ALU.is_lt))
            v(lambda e: e.scalar_tensor_tensor(out=dst[:, :], in0=sc["t0"][:, :], scalar=2 * PI, in1=dst[:, :],
                                               op0=ALU.mult, op1=ALU.add))
        reduce_to(sc["sin"], lambda t: v(lambda e: e.tensor_copy(out=t[:, :], in_=sc["ang"][:, :])))
        a(sc["sin"][:, :], sc["sin"][:, :], AF.Sin)
        reduce_to(sc["cos"], lambda t: v(lambda e: e.tensor_scalar(out=t[:, :], in0=sc["ang"][:, :], scalar1=0.5 * PI, scalar2=None,
                                                                  op0=ALU.add)))
        a(sc["cos"][:, :], sc["cos"][:, :], AF.Sin)
        P = T["Apow"]
        v(lambda e: e.tensor_tensor(out=P[:, :, 0, 0], in0=sc["mag"][:, :], in1=sc["cos"][:, :], op=ALU.mult))
        v(lambda e: e.tensor_tensor(out=P[:, :, 0, 1], in0=sc["mag"][:, :], in1=sc["sin"][:, :], op=ALU.mult))
        v(lambda e: e.tensor_scalar(out=sc["nr"][:, :], in0=P[:, :, 0, 0], scalar1=-1.0, scalar2=None, op0=ALU.add))
        v(lambda e: e.tensor_tensor(out=sc["t0"][:, :], in0=sc["are"][:, :], in1=sc["are"][:, :], op=ALU.mult))
        v(lambda e: e.tensor_tensor(out=sc["t1"][:, :], in0=sc["aim"][:, :], in1=sc["aim"][:, :], op=ALU.mult))
        v(lambda e: e.tensor_tensor(out=sc["t0"][:, :], in0=sc["t0"][:, :], in1=sc["t1"][:, :], op=ALU.add))
        v(lambda e: e.reciprocal(out=sc["den"][:, :], in_=sc["t0"][:, :]))
        v(lambda e: e.tensor_tensor(out=sc["t0"][:, :], in0=sc["nr"][:, :], in1=sc["are"][:, :], op=ALU.mult))
        v(lambda e: e.tensor_tensor(out=sc["t1"][:, :], in0=P[:, :, 0, 1], in1=sc["aim"][:, :], op=ALU.mult))
        v(lambda e: e.tensor_tensor(out=sc["t0"][:, :], in0=sc["t0"][:, :], in1=sc["t1"][:, :], op=ALU.add))
        v(lambda e: e.tensor_tensor(out=sc["cr"][:, :], in0=sc["t0"][:, :], in1=sc["den"][:, :], op=ALU.mult))
        v(lambda e: e.tensor_tensor(out=sc["t0"][:, :], in0=P[:, :, 0, 1], in1=sc["are"][:, :], op=ALU.mult))
        v(lambda e: e.tensor_tensor(out=sc["t1"][:, :], in0=sc["nr"][:, :], in1=sc["aim"][:, :], op=ALU.mult))
        v(lambda e: e.tensor_tensor(out=sc["t0"][:, :], in0=sc["t0"][:, :], in1=sc["t1"][:, :], op=ALU.subtract))
        v(lambda e: e.tensor_tensor(out=sc["ci"][:, :], in0=sc["t0"][:, :], in1=sc["den"][:, :], op=ALU.mult))
        for k in range(9):
            if k > 0:
                v(lambda e, k=k: e.tensor_tensor(out=sc["t0"][:, :], in0=P[:, :, k - 1, 0], in1=P[:, :, k - 1, 0], op=ALU.mult))
                v(lambda e, k=k: e.tensor_tensor(out=sc["t1"][:, :], in0=P[:, :, k - 1, 1], in1=P[:, :, k - 1, 1], op=ALU.mult))
                v(lambda e, k=k: e.tensor_tensor(out=P[:, :, k, 1], in0=P[:, :, k - 1, 0], in1=P[:, :, k - 1, 1], op=ALU.mult))
                v(lambda e, k=k: e.tensor_tensor(out=P[:, :, k, 0], in0=sc["t0"][:, :], in1=sc["t1"][:, :], op=ALU.subtract))
                v(lambda e, k=k: e.tensor_scalar(out=P[:, :, k, 1], in0=P[:, :, k, 1], scalar1=2.0, scalar2=None, op0=ALU.mult))
            v(lambda e, k=k: e.tensor_scalar(out=P[:, :, k, 2], in0=P[:, :, k, 1], scalar1=-1.0, scalar2=None, op0=ALU.mult))

    def ssm_branch(self, A, l, kind, t0, ncol):
        T = self.T
        hT, hr = T["hT"], T["hT_r"]
        SA, SAr = T["SA"], T["SA_r"]
        P = T["Apow"]
        RC = [T["sc_r"]]
        if kind == "s":
            for a4 in range(4):
                self.load_T(T["SsR"][:, a4], T["SsR_r"][0:16], 0, A["st_re"][l][:, a4 * 2048:(a4 + 1) * 2048], NS, 16)
                self.load_T(T["SsI"][:, a4], T["SsI_r"][0:16], 0, A["st_im"][l][:, a4 * 2048:(a4 + 1) * 2048], NS, 16)
        for jg in range(0, NCC, 4):
            pss4 = self.proj4(A["w_in"][l], 2 * CC + jg * 128, [(hT, hr, 0), (hT, hr, 16)], ncol)
            Us = []
            for j2 in range(4):
                U, Ur = self.rot("U")
                self.act(U[:, :ncol], pss4[j2][0][:, :ncol], AF.Copy, [pss4[j2][1]], [Ur])
                Us.append((U, Ur))
            for j2 in range(4):
                j = jg + j2
                U, Ur = Us[j2]
                bR, bRr = self.rot("bc")
                bI, bIr = self.rot("bc")
                t0_, t0r = self.rot("bc")
                t1_, t1r = self.rot("bc")
                cR, cRr = self.rot("cc")
                cI, cIr = self.rot("cc")
                v3 = lambda t: t[:, :].rearrange("p (a b) -> p a b", a=4)
                self.dma(v3(bR), A["b_re"][:, l, 4 * j:4 * j + 4, :], [], [bRr])
                self.dma(v3(bI), A["b_im"][:, l, 4 * j:4 * j + 4, :], [], [bIr])
                self.dma(v3(cR), A["c_re"][:, l, 4 * j:4 * j + 4, :], [], [cRr])
                self.dma(v3(cI), A["c_im"][:, l, 4 * j:4 * j + 4, :], [], [cIr])
                crb = T["sc"]["cr"][:, 4 * j:4 * j + 4].unsqueeze(2).broadcast_to([128, 4, 32])
                cib = T["sc"]["ci"][:, 4 * j:4 * j + 4].unsqueeze(2).broadcast_to([128, 4, 32])
                self.dve(lambda e, t0_=t0_, bR=bR, crb=crb: e.tensor_tensor(out=v3(t0_), in0=v3(bR), in1=crb, op=ALU.mult), [bRr] + RC, [t0r])
                self.dve(lambda e, t1_=t1_, bI=bI, cib=cib: e.tensor_tensor(out=v3(t1_), in0=v3(bI), in1=cib, op=ALU.mult), [bIr] + RC, [t1r])
                self.dve(lambda e, t0_=t0_, t1_=t1_: e.tensor_tensor(out=t0_[:, :], in0=t0_[:, :], in1=t1_[:, :], op=ALU.subtract), [t0r, t1r], [t0r])
                self.dve(lambda e, t1_=t1_, bI=bI, crb=crb: e.tensor_tensor(out=v3(t1_), in0=v3(bI), in1=crb, op=ALU.mult), [bIr] + RC, [t1r])
                self.dve(lambda e, bR=bR, cib=cib: e.tensor_tensor(out=v3(bR), in0=v3(bR), in1=cib, op=ALU.mult), [bRr] + RC, [bRr])
                self.dve(lambda e, t1_=t1_, bR=bR: e.tensor_tensor(out=t1_[:, :], in0=t1_[:, :], in1=bR[:, :], op=ALU.add), [t1r, bRr], [t1r])
                self.dve(lambda e, cI=cI: e.tensor_scalar(out=cI[:, :], in0=cI[:, :], scalar1=-1.0, scalar2=None, op0=ALU.mult), [cIr], [cIr])
                for (src, srcr, tname) in ((t0_, t0r, "tB_re"), (t1_, t1r, "tB_im")):
                    pt, pr = self.psum()
                    self.tr(pt[:, 0:128], src[:, :], T["ident"][:, :], [srcr, T["const_r"]], [pr])
                    for q4 in range(4):
                        self.act(T[tname][q4][32 * q4:32 * q4 + 32, :], pt[32 * q4:32 * q4 + 32, 0:128], AF.Copy,
                                 [pr], [T[tname + "_r"][q4]])
                yps = (T["psy"][0], T["psy_r"][0])
                for qg in range(0, 4, 2):
                    st_ = []
                    for q4 in (qg, qg + 1):
                        q = 4 * j + q4
                        pre = self.psum()
                        pim = self.psum()
                        self.mm(pre[0][:, :ncol], T["tB_re"][q4][:, :], U[:, :ncol], True, True, [T["tB_re_r"][q4], Ur], [pre[1]], True)
                        self.mm(pim[0][:, :ncol], T["tB_im"][q4][:, :], U[:, :ncol], True, True, [T["tB_im_r"][q4], Ur], [pim[1]], True)
                        sa_ = [(P[:, q, k, 0:1], P[:, q, k, 2:3], P[:, q, k, 1:2]) for k in range(9)]
                        if kind == "p":
                            hs, hsr = self.rot("HS")
                            n1 = ncol + 1
                            self.act(hs[:, 0, 1:n1], pre[0][:, :ncol], AF.Copy, [pre[1]], [hsr])
                            self.act(hs[:, 1, 1:n1], pim[0][:, :ncol], AF.Copy, [pim[1]], [hsr])
                            self.act(hs[:, 0, 0:1], T["Sre"][l][:, q:q + 1], AF.Copy, [T["S_r"][l]], [hsr])
                            self.act(hs[:, 1, 0:1], T["Sim"][l][:, q:q + 1], AF.Copy, [T["S_r"][l]], [hsr])
                            st_.append(dict(q=q, q4=q4, hs=hs, hsr=hsr, sa=sa_))
                        else:
                            hs, hsr = self.rot("hss")
                            sr_, si_ = T["SsR"][:, q // 16, q % 16, 0:NS], T["SsI"][:, q // 16, q % 16, 0:NS]
                            W = [hsr, T["SsR_r"][q], T["SsI_r"][q]] + RC
                            hre, him = hs[:, 0, 0:NS], hs[:, 1, 0:NS]
                            sa, sn, si = sa_[0]
                            self.dve(lambda e, hre=hre, sr_=sr_, pre=pre, sa=sa: e.scalar_tensor_tensor(
                                out=hre, in0=sr_, scalar=sa, in1=pre[0][:, :NS], op0=ALU.mult, op1=ALU.add), W + [pre[1]], [hsr])
                            self.dve(lambda e, hre=hre, si_=si_, sn=sn: e.scalar_tensor_tensor(
                                out=hre, in0=si_, scalar=sn, in1=hre, op0=ALU.mult, op1=ALU.add), W, [hsr])
                            self.dve(lambda e, him=him, si_=si_, pim=pim, sa=sa: e.scalar_tensor_tensor(
                                out=him, in0=si_, scalar=sa, in1=pim[0][:, :NS], op0=ALU.mult, op1=ALU.add), W + [pim[1]], [hsr])
                            self.dve(lambda e, him=him, sr_=sr_, si=si: e.scalar_tensor_tensor(
                                out=him, in0=sr_, scalar=si, in1=him, op0=ALU.mult, op1=ALU.add), W, [hsr])
                            self.act(T["SsR"][:, q // 16, q % 16, 0:NS], hre, AF.Copy, [hsr], [T["SsR_r"][q]])
                            self.act(T["SsI"][:, q // 16, q % 16, 0:NS], him, AF.Copy, [hsr], [T["SsI_r"][q]])
                            st_.append(dict(q=q, q4=q4, hsr=hsr, hre=hre, him=him))
                    if kind == "p":
                        n1 = ncol + 1
                        cur = 0
                        k = 0
                        d = 1
                        while d < n1:
                            o = 2 - cur
                            for S_ in st_:
                                hs, hsr = S_["hs"], S_["hsr"]
                                r0, i0, r1, i1 = hs[:, cur, :], hs[:, cur + 1, :], hs[:, o, :], hs[:, o + 1, :]
                                W = [hsr] + RC
                                sa, sn, si = S_["sa"][k]
                                self.dve(lambda e, r0=r0, r1=r1, sa=sa, d=d: e.scalar_tensor_tensor(
                                    out=r1[:, d:n1], in0=r0[:, 0:n1 - d], scalar=sa, in1=r0[:, d:n1], op0=ALU.mult, op1=ALU.add), W, [hsr])
                                self.dve(lambda e, i0=i0, r1=r1, sn=sn, d=d: e.scalar_tensor_tensor(
                                    out=r1[:, d:n1], in0=i0[:, 0:n1 - d], scalar=sn, in1=r1[:, d:n1], op0=ALU.mult, op1=ALU.add), W, [hsr])
                                self.dve(lambda e, i0=i0, i1=i1, sa=sa, d=d: e.scalar_tensor_tensor(
                                    out=i1[:, d:n1], in0=i0[:, 0:n1 - d], scalar=sa, in1=i0[:, d:n1], op0=ALU.mult, op1=ALU.add), W, [hsr])
                                self.dve(lambda e, r0=r0, i1=i1, si=si, d=d: e.scalar_tensor_tensor(
                                    out=i1[:, d:n1], in0=r0[:, 0:n1 - d], scalar=si, in1=i1[:, d:n1], op0=ALU.mult, op1=ALU.add), W, [hsr])
                                self.act(r1[:, 0:d], r0[:, 0:d], AF.Copy, [hsr], [hsr])
                                self.act(i1[:, 0:d], i0[:, 0:d], AF.Copy, [hsr], [hsr])
                            cur = o
                            k += 1
                            d *= 2
                        for S_ in st_:
                            hs, hsr, q = S_["hs"], S_["hsr"], S_["q"]
                            fr, fi = hs[:, cur, :], hs[:, cur + 1, :]
                            self.act(T["Sre"][l][:, q:q + 1], fr[:, ncol:n1], AF.Copy, [hsr], [T["S_r"][l]])
                            self.act(T["Sim"][l][:, q:q + 1], fi[:, ncol:n1], AF.Copy, [hsr], [T["S_r"][l]])
                            S_["hre"], S_["him"] = fr[:, 1:n1], fi[:, 1:n1]
                    for S_ in st_:
                        q4, hsr, hre, him = S_["q4"], S_["hsr"], S_["hre"], S_["him"]
                        self.act(T["tC_re"][q4][:, 32 * q4:32 * q4 + 32], cR[:, 32 * q4:32 * q4 + 32], AF.Copy, [cRr], [T["tC_re_r"][q4]])
                        self.act(T["tC_im"][q4][:, 32 * q4:32 * q4 + 32], cI[:, 32 * q4:32 * q4 + 32], AF.Copy, [cIr], [T["tC_im_r"][q4]])
                        self.mm(yps[0][:, :ncol], T["tC_re"][q4][:, :], hre, q4 == 0, False, [T["tC_re_r"][q4], hsr], [yps[1]], False)
                        self.mm(yps[0][:, :ncol], T["tC_im"][q4][:, :], him, False, q4 == 3, [T["tC_im_r"][q4], hsr], [yps[1]], q4 == 3)
                dvec = T["v2"][:, (V_SD + l) * NCC + j:(V_SD + l) * NCC + j + 1]
                yt, ytr = self.rot("sq")
                self.dve(lambda e, yt=yt, U=U, dvec=dvec, yps=yps: e.scalar_tensor_tensor(
                    out=yt[:, :ncol], in0=U[:, :ncol], scalar=dvec, in1=yps[0][:, :ncol], op0=ALU.mult, op1=ALU.add),
                    [Ur, yps[1], T["const_r"]], [ytr])
                g0, g0r = self.rot("sq")
                self.act(g0[:, :ncol], yt[:, :ncol], AF.Square, [ytr], [g0r])
                self.dve(lambda e, g0=g0: e.tensor_scalar(out=g0[:, :ncol], in0=g0[:, :ncol], scalar1=0.044715, scalar2=1.0,
                                                         op0=ALU.mult, op1=ALU.add), [g0r], [g0r])
                self.dve(lambda e, g0=g0, yt=yt: e.tensor_tensor(out=g0[:, :ncol], in0=g0[:, :ncol], in1=yt[:, :ncol], op=ALU.mult),
                         [g0r, ytr], [g0r])
                self.act(g0[:, :ncol], g0[:, :ncol], AF.Tanh, [g0r], [g0r], scale=math.sqrt(2.0 / PI))
                self.dve(lambda e, g0=g0, yt=yt: e.scalar_tensor_tensor(out=g0[:, :ncol], in0=g0[:, :ncol], scalar=1.0, in1=yt[:, :ncol],
                                                                       op0=ALU.add, op1=ALU.mult), [g0r, ytr], [g0r])
                self.act(SA[:, j, :ncol], g0[:, :ncol], AF.Copy, [g0r], [SAr[j]], scale=0.5)
        if kind == "s":
            for a4 in range(4):
                self.store_T(A["sr_s"][l][:, a4 * 2048:(a4 + 1) * 2048], T["SsR"][:, a4], T["SsR_r"][0:16], 0, NS, 16)
                self.store_T(A["si_s"][l][:, a4 * 2048:(a4 + 1) * 2048], T["SsI"][:, a4], T["SsI_r"][0:16], 0, NS, 16)
        elif t0 + ncol == SEQ:
            for (src, oname) in ((T["Sre"][l], "sr_p"), (T["Sim"][l], "si_p")):
                pt, pr = self.psum()
                self.tr(pt[:64, 0:128], src[:, :], T["ident"][:, :], [T["S_r"][l], T["const_r"]], [pr])
                stg, sr = self.rot("stg")
                self.act(stg[:64, 0:128], pt[:64, 0:128], AF.Copy, [pr], [sr])
                self.dma(A[oname][l], stg[:64, 0:128], [sr], [])
        GB, GBr = T["M2"], T["M2_r"]
        for j in range(NCC):
            self.act(GB[:, j, :ncol], SA[:, j, :ncol], AF.Copy, [SAr[j]], [GBr[j]])
        SO, SOr = SA, SAr
        for og in range(0, NCC, 4):
            wt, wr = self.wget_tile(A["w_glu"][l], 0, 16, og * 128, 512)
            for o4 in range(4):
                o = og + o4
                ps = self.proj_group(wt, wr, o4, 16, GB, GBr, ncol)
                sg, sgr = self.rot("sq")
                self.act(sg[:, :ncol], ps[0][:, :ncol], AF.Sigmoid, [ps[1]], [sgr])
                self.dve(lambda e, o=o, sg=sg: e.tensor_tensor(out=SO[:, o, :ncol], in0=SA[:, o, :ncol], in1=sg[:, :ncol], op=ALU.mult),
                         [SAr[o], sgr], [SOr[o]])
        self.norm_to(T["M2"], T["M2_r"], SO, SOr, NCC, ncol, T["v2"][:, (V_BGS + l) * NCC:(V_BGS + l + 1) * NCC], CC)

    def conv_branch(self, A, l, kind, t0, ncol):
        T = self.T
        hT, hr = T["hT"], T["hT_r"]
        SA, SAr = T["SA"], T["SA_r"]
        cw = T["convw"]
        v2 = T["v2"]
        if kind == "s":
            GS, GSr = T["GS"], T["GS_r"]
        for ig in range(0, NCC, 2):
            wv, wvr = self.wget_tile(A["w_in"][l], 0, 32, ig * 128, 256)
            pvs = [self.proj_group(wv, wvr, i2, 32, hT, hr, ncol) for i2 in range(2)]
            wg, wgr = self.wget_tile(A["w_in"][l], 0, 32, CC + ig * 128, 256)
            taps = []
            for i2 in range(2):
                i = ig + i2
                pv = pvs[i2]
                pg = self.proj_group(wg, wgr, i2, 32, hT, hr, ncol)
                sg, sgr = self.rot("sq")
                self.act(sg[:, :ncol], pg[0][:, :ncol], AF.Sigmoid, [pg[1]], [sgr])
                wbase = (l * NCC + i) * 31
                bias = v2[:, (V_CB + l) * NCC + i:(V_CB + l) * NCC + i + 1]
                if kind == "p":
                    gx, gxr = self.rot("GX")
                    self.act(gx[:, 0:30], T["hist"][l][:, i, :], AF.Copy, [T["hist_r"][l][i]], [gxr])
                    self.dve(lambda e, gx=gx, pv=pv, sg=sg: e.tensor_tensor(out=gx[:, 30:30 + ncol], in0=pv[0][:, :ncol],
                                                                            in1=sg[:, :ncol], op=ALU.mult),
                             [pv[1], sgr], [gxr])
                    self.act(T["hist"][l][:, i, :], gx[:, ncol:ncol + 30], AF.Copy, [gxr], [T["hist_r"][l][i]])
                    co = SA[:, i, :ncol]
                    self.dve(lambda e, gx=gx, co=co, wbase=wbase, bias=bias: e.tensor_scalar(
                        out=co, in0=gx[:, 0:ncol], scalar1=cw[:, wbase:wbase + 1], scalar2=bias, op0=ALU.mult, op1=ALU.add),
                        [gxr, T["const_r"]], [SAr[i]])
                    taps.append((gx, gxr, co, wbase, i))
                else:
                    self.dve(lambda e, i=i, pv=pv, sg=sg: e.tensor_tensor(out=GS[:, i, :], in0=pv[0][:, :ncol], in1=sg[:, :ncol], op=ALU.mult),
                             [pv[1], sgr], [GSr[i]])
                    self.dve(lambda e, i=i, wbase=wbase, bias=bias: e.tensor_scalar(
                        out=SA[:, i, :ncol], in0=GS[:, i, :], scalar1=cw[:, wbase + 30:wbase + 31], scalar2=bias, op0=ALU.mult, op1=ALU.add),
                        [GSr[i], T["const_r"]], [SAr[i]])
            for k in range(1, 31):
                for (gx, gxr, co, wbase, i) in taps:
                    self.dve(lambda e, gx=gx, co=co, k=k, wbase=wbase: e.scalar_tensor_tensor(
                        out=co, in0=gx[:, k:k + ncol], scalar=cw[:, wbase + k:wbase + k + 1], in1=co, op0=ALU.mult, op1=ALU.add),
                        [gxr, SAr[i], T["const_r"]], [SAr[i]])
        if kind == "s":
            for rb in range(4):
                rows = A["cconv"][l, rb * 4:(rb + 1) * 4, :, :].rearrange("b k c -> (b k) c")
                for i in range(NCC):
                    if i % 4 == 0:
                        stg, sr = self.rot("stg")
                        self.dma(stg[:120, 0:512], rows[:, i * 128:(i + 4) * 128], [], [sr])
                    pt, pr = self.psum()
                    self.tr(pt[:, 0:120], stg[:120, (i % 4) * 128:(i % 4 + 1) * 128], T["ident"][:120, :120], [sr, T["const_r"]], [pr])
                    wbase = (l * NCC + i) * 31
                    tm, tmr = self.rot("sq")
                    wbc = cw[:, wbase:wbase + 30].unsqueeze(1).broadcast_to([128, 4, 30])
                    self.dve(lambda e, tm=tm, pt=pt, wbc=wbc: e.tensor_tensor(
                        out=tm[:, 0:120].rearrange("p (b k) -> p b k", b=4), in0=pt[:, 0:120].rearrange("p (b k) -> p b k", b=4),
                        in1=wbc, op=ALU.mult), [pr, T["const_r"]], [tmr])
                    self.dve(lambda e, tm=tm: e.tensor_reduce(out=tm[:, 128:132], in_=tm[:, 0:120].rearrange("p (b k) -> p b k", b=4),
                                                              axis=AX.X, op=ALU.add), [tmr], [tmr])
                    self.dve(lambda e, tm=tm, i=i, rb=rb: e.tensor_tensor(out=SA[:, i, rb * 4:rb * 4 + 4], in0=SA[:, i, rb * 4:rb * 4 + 4],
                                                                          in1=tm[:, 128:132], op=ALU.add), [tmr, SAr[i]], [SAr[i]])
            self.dma(A["cv_s"][l, :, 0:29, :], A["cconv"][l, :, 1:30, :], [], [])
            self.store_T(A["cv_s"][l, :, 29, :], GS, GSr, 0, NS, NCC)
        elif t0 + ncol == SEQ:
            self.store_T(A["cv_p"][l], T["hist"][l], T["hist_r"][l], 0, 30, NCC)
        ps, pr = self.psum()
        for i in range(NCC):
            self.mm(ps[:, :ncol], T["ones"][:, :], SA[:, i, :ncol], i == 0, i == NCC - 1, [SAr[i], T["const_r"]], [pr], i == NCC - 1)
        mean, meanr = T["mean"], T["mean_r"]
        self.act(mean[:, :ncol], ps[:, :ncol], AF.Copy, [pr], [meanr], scale=1.0 / CC)
        rb_, rbr = T["rstd2"], T["rstd2_r"]
        self.rstd_bc(SA, SAr, NCC, ncol, rb_, rbr, CC, sub_mean=(mean, meanr))
        for i in range(NCC):
            self.dve(lambda e, i=i: e.tensor_tensor(out=SA[:, i, :ncol], in0=SA[:, i, :ncol], in1=mean[:, :ncol], op=ALU.subtract),
                     [SAr[i], meanr], [SAr[i]])
            self.dve(lambda e, i=i: e.tensor_tensor(out=SA[:, i, :ncol], in0=SA[:, i, :ncol], in1=rb_[:, :ncol], op=ALU.mult),
                     [SAr[i], rbr], [SAr[i]])
            lg = v2[:, (V_LNG + l) * NCC + i:(V_LNG + l) * NCC + i + 1]
            lb = v2[:, (V_LNB + l) * NCC + i:(V_LNB + l) * NCC + i + 1]
            self.act(SA[:, i, :ncol], SA[:, i, :ncol], AF.Silu, [SAr[i], T["const_r"]], [SAr[i]], bias=lb, scale=lg)
        self.norm_to(T["M1"], T["M1_r"], SA, SAr, NCC, ncol, v2[:, (V_BGC + l) * NCC:(V_BGC + l + 1) * NCC], CC)

    def xattn(self, A, l, kind, t0, ncol):
        T = self.T
        xT, xr = T["xT"], T["xT_r"]
        hT, hr = T["hT"], T["hT_r"]
        self.norm_to(hT, hr, xT, xr, ND, ncol, T["v4"][:, (V_X + l) * ND:(V_X + l + 1) * ND], D)
        sc_ = 1.0 / math.sqrt(128.0)
        oT, oTr = T["oT"], T["oT_r"]
        if kind == "p":
            qT, qTr = T["qT"], T["qT_r"]
        else:
            qT, qTr = T["qTf"], T["qTf_r"]
        pss = self.proj4(A["w_xq"][l], 0, [(hT, hr, 0), (hT, hr, 16)], ncol)
        for h in range(4):
            self.act(qT[:, h, :ncol], pss[h][0][:, :ncol], AF.Copy, [pss[h][1]], [qTr[h]], scale=sc_)
        if kind == "p":
            for tb in range(0, ncol, 128):
                for h in range(4):
                    sp_, spr = self.psum()
                    self.mm(sp_[:, :NMEM], qT[:, h, tb:tb + 128], T["KT"][l][:, h, :], True, True, [qTr[h], T["KT_r"][l]], [spr], True)
                    st, str_ = self.rot("st")
                    self.dve(lambda e, st=st, sp_=sp_: e.reduce_max(out=st[:, 0:1], in_=sp_[:, :NMEM], axis=AX.X), [spr], [str_])
                    self.dve(lambda e, st=st: e.tensor_scalar(out=st[:, 1:2], in0=st[:, 0:1], scalar1=-1.0, scalar2=None, op0=ALU.mult),
                             [str_], [str_])
                    ex, exr = self.rot("sq")
                    self.act(ex[:, :NMEM], sp_[:, :NMEM], AF.Exp, [spr, str_], [exr], bias=st[:, 1:2])
                    self.dve(lambda e, st=st, ex=ex: e.tensor_reduce(out=st[:, 2:3], in_=ex[:, :NMEM], axis=AX.X, op=ALU.add), [exr, str_], [str_])
                    self.dve(lambda e, st=st: e.reciprocal(out=st[:, 3:4], in_=st[:, 2:3]), [str_], [str_])
                    self.dve(lambda e, ex=ex, st=st: e.tensor_scalar(out=ex[:, :NMEM], in0=ex[:, :NMEM], scalar1=st[:, 3:4], scalar2=None,
                                                                    op0=ALU.mult), [exr, str_], [exr])
                    wT, wTr = self.rot("wT")
                    pb, pbr = self.psum()
                    for mb in range(2):
                        self.tr(pb[:, mb * 128:(mb + 1) * 128], ex[:, mb * 128:(mb + 1) * 128], T["ident"][:, :], [exr, T["const_r"]], [pbr])
                    self.act(wT[:, :, :], pb[:, 0:256].rearrange("p (a b) -> p a b", a=2), AF.Copy, [pbr], [wTr])
                    op_, opr = self.psum()
                    for mb in range(2):
                        self.mm(op_[:, :128], T["Vb"][l][:, mb, h * 128:(h + 1) * 128], wT[:, mb, :], mb == 0, mb == 1,
                                [T["Vb_r"][l], wTr], [opr], mb == 1)
                    self.act(oT[:, h, tb:tb + 128], op_[:, :128], AF.Copy, [opr], [oTr[h]])
        else:
            for b in range(NS):
                kb, kbr = self.rot("kv")
                self.dma(kb[:, :, :], A["ck"][l, b].rearrange("(a p) n -> p a n", p=128), [], [kbr])
                sps = [self.psum(), self.psum()]
                for h in range(4):
                    pt, pr = self.psum()
                    for mb in range(2):
                        self.tr(pt[:, mb * 128:(mb + 1) * 128], kb[:, mb, h * 128:(h + 1) * 128], T["ident"][:, :], [kbr, T["const_r"]], [pr])
                    kth, ktr = self.rot("kth")
                    self.act(kth[:, :], pt[:, 0:256], AF.Copy, [pr], [ktr])
                    self.mm(sps[h // 2][0][0:1, (h % 2) * 256:(h % 2 + 1) * 256], qT[:, h, b:b + 1], kth[:, :], True, True,
                            [qTr[h], ktr], [sps[h // 2][1]], True)
                vb_, vbr = self.rot("kv")
                self.dma(vb_[:, :, :], A["cvv"][l, b].rearrange("(a p) n -> p a n", p=128), [], [vbr])
                s1, s1r = self.rot("s1")
                sv = s1[0:1, 0:1024].rearrange("p (h m) -> p h m", h=4)
                for hp in range(2):
                    spt, spr = sps[hp]
                    pin = spt[0:1, 0:512].rearrange("p (h m) -> p h m", h=2)
                    mx = s1[0:1, 1024 + 2 * hp:1026 + 2 * hp]
                    svh = s1[0:1, hp * 512:(hp + 1) * 512].rearrange("p (h m) -> p h m", h=2)
                    self.dve(lambda e, mx=mx, pin=pin: e.tensor_reduce(out=mx, in_=pin, axis=AX.X, op=ALU.max), [spr], [s1r])
                    self.dve(lambda e, svh=svh, pin=pin, mx=mx: e.tensor_tensor(
                        out=svh, in0=pin, in1=mx.unsqueeze(2).broadcast_to([1, 2, 256]), op=ALU.subtract), [spr, s1r], [s1r])
                self.act(s1[0:1, 0:1024], s1[0:1, 0:1024], AF.Exp, [s1r], [s1r])
                self.dve(lambda e, s1=s1, sv=sv: e.tensor_reduce(out=s1[0:1, 1028:1032], in_=sv, axis=AX.X, op=ALU.add), [s1r], [s1r])
                self.dve(lambda e, s1=s1: e.reciprocal(out=s1[0:1, 1028:1032], in_=s1[0:1, 1028:1032]), [s1r], [s1r])
                self.dve(lambda e, s1=s1, sv=sv: e.tensor_tensor(out=sv, in0=sv, in1=s1[0:1, 1028:1032].unsqueeze(2).broadcast_to([1, 4, 256]),
                                                                 op=ALU.mult), [s1r], [s1r])
                pt, pr = self.psum()
                for c in range(8):
                    self.tr(pt[:, c:c + 1], s1[0:1, c * 128:(c + 1) * 128], T["ident"][0:1, 0:1], [s1r, T["const_r"]], [pr])
                wc, wcr = self.rot("st")
                self.act(wc[:, 0:8], pt[:, 0:8], AF.Copy, [pr], [wcr])
                op_, opr = self.psum()
                for h in range(4):
                    for mb in range(2):
                        self.mm(op_[:, h:h + 1], vb_[:, mb, h * 128:(h + 1) * 128], wc[:, h * 2 + mb:h * 2 + mb + 1], mb == 0, mb == 1,
                                [vbr, wcr], [opr], mb == 1)
                for h in range(4):
                    self.act(oT[:, h, b:b + 1], op_[:, h:h + 1], AF.Copy, [opr], [oTr[h]])
        for dg in range(0, ND, 16):
            wt, wr = self.wget_tile(A["w_xo"][l], 0, 4, dg * 128, 2048)
            for d16 in range(16):
                d = dg + d16
                ps = self.proj_group(wt, wr, d16, 4, oT, oTr, ncol)
                self.dve(lambda e, d=d, ps=ps: e.tensor_tensor(out=xT[:, d, :ncol], in0=ps[0][:, :ncol], in1=xT[:, d, :ncol], op=ALU.add),
                         [ps[1], xr[d]], [xr[d]])

    def mlp(self, A, l, ncol):
        T = self.T
        xT, xr = T["xT"], T["xT_r"]
        hT, hr = T["hT"], T["hT_r"]
        self.norm_to(hT, hr, xT, xr, ND, ncol, T["v4"][:, (V_FFN + l) * ND:(V_FFN + l + 1) * ND], D)
        aT, aTr = T["M1"], T["M1_r"]
        banks = [(T["ps"][i], T["ps_r"][i]) for i in range(6)] + [(T["psy"][0], T["psy_r"][0]), (T["psbf"], T["psb_r"][0])]
        FB = 8
        for fb in range(0, FF // 128, FB):
            for kt in range(4):
                wt, wr = self.wget_tile(A["w_up"][l], kt * 1024, 8, fb * 128, 1024)
                for o in range(FB):
                    pt, pr = banks[o]
                    for k in range(8):
                        c = kt * 8 + k
                        self.mm(pt[:, :ncol], wt[:, k, o * 128:(o + 1) * 128], hT[:, c, :ncol], c == 0, c == 31,
                                [wr, hr[c]], [pr], True)
            for o in range(FB):
                pt, pr = banks[o]
                rl, rlr = self.rot("sq")
                self.act(rl[:, :ncol], pt[:, :ncol], AF.Relu, [pr], [rlr])
                self.dve(lambda e, o=o, rl=rl: e.tensor_tensor(out=aT[:, o, :ncol], in0=rl[:, :ncol], in1=rl[:, :ncol], op=ALU.mult),
                         [rlr], [aTr[o]])
            for dg in range(0, ND, 8):
                wt, wr = self.wget_tile(A["w_down"][l], fb * 128, FB, dg * 128, 1024)
                for d8 in range(8):
                    d = dg + d8
                    ps = self.proj_group(wt, wr, d8, FB, aT, aTr, ncol)
                    self.dve(lambda e, d=d, ps=ps: e.tensor_tensor(out=xT[:, d, :ncol], in0=ps[0][:, :ncol], in1=xT[:, d, :ncol], op=ALU.add),
                             [ps[1], xr[d]], [xr[d]])

    def final_out(self, A, kind, t0, ncol):
        T = self.T
        xT, xr = T["xT"], T["xT_r"]
        rb, rr = T["rstd"], T["rstd_r"]
        self.rstd_bc(xT, xr, ND, ncol, rb, rr, D)
        g = T["v4"][:, V_FINAL * ND:(V_FINAL + 1) * ND]
        for c in range(ND):
            self.dve(lambda e, c=c: e.scalar_tensor_tensor(out=xT[:, c, :ncol], in0=xT[:, c, :ncol], scalar=g[:, c:c + 1], in1=rb[:, :ncol],
                                                           op0=ALU.mult, op1=ALU.mult), [xr[c], rr, T["const_r"]], [xr[c]])
        if kind == "p":
            for r0 in range(0, ncol, 128):
                self.store_T(A["y_p"][t0 + r0:t0 + r0 + 128, :], xT, xr, r0, 128, ND)
        else:
            self.store_T(A["y_s"][:, :], xT, xr, 0, NS, ND)


def build_program():
    nc = bass.Bass("TRN2", target_bir_lowering=False)

    def din(name, shape):
        return nc.dram_tensor(name, list(shape), F32, kind="ExternalInput").ap()

    def dout(name, shape):
        return nc.dram_tensor(name, list(shape), F32, kind="ExternalOutput").ap()

    A = {}
    A["xp"] = din("xp", [SEQ, D])
    A["xs"] = din("xs", [NS, D])
    A["mem"] = din("mem", [NMEM, D])
    A["cconv"] = din("cconv", [L, NS, 30, CC])
    A["st_re"] = din("st_re", [L, NS, 8192])
    A["st_im"] = din("st_im", [L, NS, 8192])
    A["ck"] = din("ck", [L, NS, NMEM, 512])
    A["cvv"] = din("cvv", [L, NS, NMEM, 512])
    A["ident"] = din("ident", [128, 128])
    A["vecs4096"] = din("vecs4096", [128, 9 * ND])
    A["vecs2048"] = din("vecs2048", [128, 12 * NCC])
    A["convw"] = din("convw", [128, L * NCC * 31])
    A["a_re"] = din("a_re", [128, L, 64])
    A["a_im"] = din("a_im", [128, L, 64])
    A["log_dt"] = din("log_dt", [128, L, 64])
    A["b_re"] = din("b_re", [128, L, 64, 32])
    A["b_im"] = din("b_im", [128, L, 64, 32])
    A["c_re"] = din("c_re", [128, L, 64, 32])
    A["c_im"] = din("c_im", [128, L, 64, 32])
    A["w_in"] = din("w_in", [L, D, 3 * CC])
    A["w_glu"] = din("w_glu", [L, CC, CC])
    A["w_out"] = din("w_out", [L, D, D])
    A["w_xq"] = din("w_xq", [L, D, 512])
    A["w_xk"] = din("w_xk", [L, D, 512])
    A["w_xv"] = din("w_xv", [L, D, 512])
    A["w_xo"] = din("w_xo", [L, 512, D])
    A["w_up"] = din("w_up", [L, D, FF])
    A["w_down"] = din("w_down", [L, FF, D])
    A["y_p"] = dout("y_p", [SEQ, D])
    A["y_s"] = dout("y_s", [NS, D])
    A["mk"] = dout("mk", [L, NMEM, 512])
    A["mv"] = dout("mv", [L, NMEM, 512])
    A["cv_p"] = dout("cv_p", [L, 30, CC])
    A["sr_p"] = dout("sr_p", [L, 64, 128])
    A["si_p"] = dout("si_p", [L, 64, 128])
    A["cv_s"] = dout("cv_s", [L, NS, 30, CC])
    A["sr_s"] = dout("sr_s", [L, NS, 8192])
    A["si_s"] = dout("si_s", [L, NS, 8192])

    es = ExitStack()
    T = {}

    def sb(name, shape, dt=F32):
        return es.enter_context(nc.sbuf_tensor("sb_" + name, list(shape), dt))

    def add(name, shape, dt=F32, nres=None):
        T[name] = sb(name, shape, dt)
        T[name + "_r"] = Res() if nres is None else RL(nres)

    def addn(name, n, shape, dt=F32):
        T[name] = [sb("%s%d" % (name, i), shape, dt) for i in range(n)]
        T[name + "_r"] = RL(n)

    T["const_r"] = Res()
    T["ones"] = sb("ones", [128, 128])
    T["eps"] = sb("eps", [128, 1])
    T["negpi"] = sb("negpi", [128, 1])
    T["ident"] = sb("ident", [128, 128])
    T["v4"] = sb("v4", [128, 9 * ND])
    T["v2"] = sb("v2", [128, 12 * NCC])
    T["convw"] = sb("convw", [128, L * NCC * 31])
    add("xT", [128, ND, TT], F32, ND)
    add("hT", [128, ND, TT], BF16, ND)
    add("M1", [128, NCC, TT], BF16, NCC)
    T["M2"] = T["hT"][:, 0:NCC, :]
    T["M2_r"] = T["hT_r"][0:NCC]
    add("SA", [128, NCC, TT], F32, NCC)
    add("rstd", [128, TT])
    add("rstd2", [128, TT])
    add("mean", [128, TT])
    add("qT", [128, 4, TT], BF16, 4)
    add("qTf", [128, 4, NS], F32, 4)
    add("oT", [128, 4, TT], BF16, 4)
    add("GS", [128, NCC, NS], F32, NCC)
    T["KT"] = [sb("KT%d" % l, [128, 4, NMEM], BF16) for l in range(L)]
    T["KT_r"] = RL(L)
    T["Vb"] = [sb("Vb%d" % l, [128, 2, 512], BF16) for l in range(L)]
    T["Vb_r"] = RL(L)
    T["hist"] = [sb("hist%d" % l, [128, NCC, 30]) for l in range(L)]
    T["hist_r"] = [RL(NCC) for l in range(L)]
    T["Sre"] = [sb("Sre%d" % l, [128, 64]) for l in range(L)]
    T["Sim"] = [sb("Sim%d" % l, [128, 64]) for l in range(L)]
    T["S_r"] = RL(L)
    T["sc"] = {k: sb("sc_" + k, [128, 64]) for k in ("are", "aim", "dt", "t0", "t1", "mag", "ang", "sin", "cos", "nr", "den", "cr", "ci")}
    T["sc_r"] = Res()
    T["sc_i"] = sb("sc_i", [128, 64], mybir.dt.int32)
    T["Apow"] = sb("Apow", [128, 64, 9, 3])
    for nm in ("tB_re", "tB_im", "tC_re", "tC_im"):
        addn(nm, 4, [128, 128])
    addn("wslot", NSLOT, [128, SLOT_ELEMS], BF16)
    addn("stg", 2, [128, 512])
    addn("sq", 3, [128, 512])
    addn("bc", 4, [128, 128])
    addn("cc", 2, [128, 128])
    addn("hss", 2, [128, 2, NS])
    addn("U", 4, [128, TT])
    addn("HS", 2, [128, 4, TT + 1])
    T["SsR"] = T["HS"][0][:, 0:4, 0:256].rearrange("p a (b c) -> p a b c", c=NS)
    T["SsI"] = T["HS"][1][:, 0:4, 0:256].rearrange("p a (b c) -> p a b c", c=NS)
    T["SsR_r"] = [T["HS_r"][0]] * 64
    T["SsI_r"] = [T["HS_r"][1]] * 64
    addn("GX", 2, [128, 30 + TT])
    addn("st", 4, [128, 8])
    addn("wT", 2, [128, 2, 128], BF16)
    addn("kv", 1, [128, 2, 512])
    addn("kth", 2, [128, 256])
    addn("s1", 1, [1, 1040])
    T["ps"] = [es.enter_context(nc.psum_tensor("ps%d" % i, [128, 512], F32)) for i in range(6)]
    T["ps_r"] = RL(6)
    T["psy"] = [es.enter_context(nc.psum_tensor("psy0", [128, 512], F32))]
    T["psy_r"] = RL(1)
    T["psbf"] = es.enter_context(nc.psum_tensor("psb0", [128, 512], F32))
    T["psb_r"] = RL(1)

    dry = Builder(nc, Prog(nc, True), T, True, None)
    dry.build(A)
    wlist = dry.wrec
    def reset(o):
        if isinstance(o, Res):
            o.w = None
            o.r = {}
        elif isinstance(o, (list, tuple)):
            for x in o:
                reset(x)
    for k, v in T.items():
        if k.endswith("_r"):
            reset(v)
    prog = Prog(nc, False)
    es.enter_context(prog.es)
    real = Builder(nc, prog, T, False, wlist)
    real.build(A)
    with nc.Block() as block:
        @block.tensor
        def _(e):
            for f in prog.q["pe"]:
                f(e)

        @block.scalar
        def _(e):
            for f in prog.q["act"]:
                f(e)

        @block.vector
        def _(e):
            for f in prog.q["dve"]:
                f(e)

        @block.gpsimd
        def _(e):
            for f in prog.q["pool"]:
                f(e)

        @block.sync
        def _(e):
            for f in prog.q["sp"]:
                f(e)
    es.close()
    return nc, prog


def _fm(v, n):
    return np.ascontiguousarray(v.reshape(n, 128).T)


def kernel(**inp):
    f = lambda k: np.asarray(inp[k], dtype=np.float32)
    x_prompt, x_sample, mem_prompt = f("x_prompt"), f("x_sample"), f("mem_prompt")
    cache_conv, st_re, st_im = f("cache_conv"), f("state_ssm_re"), f("state_ssm_im")
    ck, cv = f("cache_mem_k"), f("cache_mem_v")
    v4 = []
    for nm in ("norm_mix_g", "norm_x_g", "norm_mem_g", "norm_ffn_g"):
        for l in range(L):
            v4.append(_fm(f(nm)[l], ND))
    v4.append(_fm(f("norm_final_g"), ND))
    vecs4096 = np.ascontiguousarray(np.concatenate(v4, axis=1))
    v2 = []
    for nm in ("conv_b", "conv_ln_g", "conv_ln_b", "ssm_d", "branch_g_conv", "branch_g_ssm"):
        for l in range(L):
            v2.append(_fm(f(nm)[l], NCC))
    vecs2048 = np.ascontiguousarray(np.concatenate(v2, axis=1))
    cw = f("conv_w")
    convw = np.ascontiguousarray(cw.reshape(L, 31, NCC, 128).transpose(3, 0, 2, 1).reshape(128, L * NCC * 31))

    def gn(a):
        return np.ascontiguousarray(a.reshape(L, 64, 2, 64).transpose(2, 3, 0, 1).reshape(128, L, 64))
    a_re, a_im = gn(f("ssm_a_re")), gn(f("ssm_a_im"))
    ldt = f("ssm_log_dt")
    log_dt = gn(np.broadcast_to(ldt[:, :, None], (L, 128, 64)))

    def bd_b(b):
        o = np.zeros((2, 64, L, 64, 2, 16), np.float32)
        bb = b.reshape(L, 64, 2, 64, 16)
        for g2 in range(2):
            o[g2, :, :, :, g2, :] = bb[:, :, g2, :, :].transpose(2, 0, 1, 3)
        return np.ascontiguousarray(o.reshape(128, L, 64, 32))

    def bd_c(c):
        o = np.zeros((2, 64, L, 64, 2, 16), np.float32)
        cc = c.reshape(L, 64, 2, 16, 64)
        for g2 in range(2):
            o[g2, :, :, :, g2, :] = cc[:, :, g2, :, :].transpose(3, 0, 1, 2)
        return np.ascontiguousarray(o.reshape(128, L, 64, 32))
    b_re, b_im = bd_b(f("ssm_b_re")), bd_b(f("ssm_b_im"))
    c_re, c_im = bd_c(f("ssm_c_re")), bd_c(f("ssm_c_im"))
    shared = dict(ident=np.eye(128, dtype=np.float32), vecs4096=vecs4096, vecs2048=vecs2048, convw=convw,
                  a_re=a_re, a_im=a_im, log_dt=log_dt, b_re=b_re, b_im=b_im, c_re=c_re, c_im=c_im,
                  w_in=f("w_in"), w_glu=f("w_glu"), w_out=f("w_out"), w_xq=f("w_xq"), w_xk=f("w_xk"),
                  w_xv=f("w_xv"), w_xo=f("w_xo"), w_up=f("w_up"), w_down=f("w_down"))
    in_maps = []
    for c in range(8):
        b = c % 4
        s0 = c * NS
        m = dict(shared)
        m["xp"] = np.ascontiguousarray(x_prompt[b])
        m["xs"] = np.ascontiguousarray(x_sample[s0:s0 + NS, 0])
        m["mem"] = np.ascontiguousarray(mem_prompt[b])
        m["cconv"] = np.ascontiguousarray(cache_conv[:, s0:s0 + NS])
        m["st_re"] = np.ascontiguousarray(st_re[:, s0:s0 + NS].reshape(L, NS, 8192))
        m["st_im"] = np.ascontiguousarray(st_im[:, s0:s0 + NS].reshape(L, NS, 8192))
        m["ck"] = np.ascontiguousarray(ck[:, s0:s0 + NS].reshape(L, NS, NMEM, 512))
        m["cvv"] = np.ascontiguousarray(cv[:, s0:s0 + NS].reshape(L, NS, NMEM, 512))
        in_maps.append(m)
    nc, prog = build_program()
    res = run_bass_kernel_spmd(nc, in_maps, core_ids=list(range(8)))
    R = res.results
    y_prompt = np.stack([R[b]["y_p"] for b in range(4)]).astype(np.float32)
    y_sample = np.concatenate([R[c]["y_s"] for c in range(8)], axis=0).reshape(128, 1, D).astype(np.float32)
    mk = np.stack([R[b]["mk"] for b in range(4)], axis=1).reshape(L, 4, NMEM, 4, 128)
    mv = np.stack([R[b]["mv"] for b in range(4)], axis=1).reshape(L, 4, NMEM, 4, 128)
    cvp = np.stack([R[b]["cv_p"] for b in range(4)], axis=1)
    srp = np.stack([R[b]["sr_p"] for b in range(4)], axis=1).reshape(L, 4, 128, 64)
    sip = np.stack([R[b]["si_p"] for b in range(4)], axis=1).reshape(L, 4, 128, 64)
    cvs = np.concatenate([R[c]["cv_s"] for c in range(8)], axis=1)
    srs = np.concatenate([R[c]["sr_s"] for c in range(8)], axis=1).reshape(L, 128, 128, 64)
    sis = np.concatenate([R[c]["si_s"] for c in range(8)], axis=1).reshape(L, 128, 128, 64)
    return (y_prompt, y_sample, mk.astype(np.float32), mv.astype(np.float32), cvp.astype(np.float32),
            srp.astype(np.float32), sip.astype(np.float32), cvs.astype(np.float32), srs.astype(np.float32),
            sis.astype(np.float32))
```
